# Optimizing a Trainium2 kernel written in Bass

```python
import math
import jax, jax.numpy as jnp
from jax import lax
import numpy as np

D_MODEL = 2048
BATCH = 4
SEQ = 2048
DEPTH = 4
DEC_BATCH = 128
DEC_SEQ = 1
PAST_LEN = 16384
PAGE_SIZE = 128

N_MIXERS = 2
N_A_LAYERS = (DEPTH + 1) // 2
N_B_LAYERS = DEPTH // 2
GDN_HEAD_K = 128
GDN_HEAD_V = 128
GDN_K_HEADS = D_MODEL // GDN_HEAD_K
GDN_V_HEADS = 2 * GDN_K_HEADS
GDN_KEY_DIM = GDN_K_HEADS * GDN_HEAD_K
GDN_VALUE_DIM = GDN_V_HEADS * GDN_HEAD_V
GDN_CONV_DIM = 2 * GDN_KEY_DIM + GDN_VALUE_DIM
GDN_CONV_WIDTH = 4
GDN_CHUNK = 64
SC_WIDTH = 3
SC_DIM = D_MODEL
D_FF = 4 * D_MODEL
N_MOD = 6
NORM_EPS = 1e-6

kernel_name = 'hybrid_gdn_shortconv_adaln_decode_step'


def rms_norm(x, gain):
    x32 = x.astype(jnp.float32)
    y = x32 * lax.rsqrt(jnp.mean(x32 * x32, axis=-1, keepdims=True) + NORM_EPS)
    return (y * gain.astype(jnp.float32)).astype(x.dtype)


def l2_normalize(x):
    return x * lax.rsqrt(jnp.sum(x * x, axis=-1, keepdims=True) + NORM_EPS)


def causal_depthwise_conv(x, buf, w):
    width = w.shape[0]
    t_len = x.shape[1]
    xx = jnp.concatenate([buf.astype(x.dtype), x], axis=1)
    y = xx[:, 0:t_len] * w[0]
    for j in range(1, width):
        y = y + xx[:, j:j + t_len] * w[j]
    return y, xx[:, t_len:].astype(buf.dtype)


def gated_delta_rule(q, k, v, g, beta, s0):
    bsz, t_len, n_h, _ = q.shape
    chunk = min(GDN_CHUNK, t_len)
    n_chunks = -(-t_len // chunk)
    pad = n_chunks * chunk - t_len

    def blocks(a):
        a = jnp.pad(a, [(0, 0), (0, pad)] + [(0, 0)] * (a.ndim - 2))
        a = a.reshape((bsz, n_chunks, chunk) + a.shape[2:])
        return jnp.moveaxis(a, (1, 3), (0, 2))

    qc, kc, vc, gc, bc = blocks(q), blocks(k), blocks(v), blocks(g), blocks(beta)
    gcum = jnp.cumsum(gc, axis=-1)
    idx = jnp.arange(chunk)
    incl = idx[:, None] >= idx[None, :]
    strict = idx[:, None] > idx[None, :]
    diff = gcum[..., :, None] - gcum[..., None, :]
    decay = jnp.where(incl, jnp.exp(jnp.where(incl, diff, 0.0)), 0.0)
    kb = kc * bc[..., None]
    a_mat = jnp.where(strict, jnp.einsum('nbhid,nbhjd->nbhij', kb, kc) * decay, 0.0)
    eye = jnp.eye(chunk, dtype=jnp.float32)
    t_mat = lax.linalg.triangular_solve(eye + a_mat, jnp.broadcast_to(eye, a_mat.shape),
                                        left_side=True, lower=True, unit_diagonal=True)
    u = jnp.einsum('nbhij,nbhjd->nbhid', t_mat, vc * bc[..., None])
    w = jnp.einsum('nbhij,nbhjd->nbhid', t_mat, kb * jnp.exp(gcum)[..., None])
    qk = jnp.einsum('nbhid,nbhjd->nbhij', qc, kc) * decay

    def step(s, xs):
        q_i, k_i, u_i, w_i, g_i, qk_i = xs
        v_new = u_i - jnp.einsum('bhck,bhkv->bhcv', w_i, s)
        o_i = (jnp.einsum('bhck,bhkv->bhcv', q_i * jnp.exp(g_i)[..., None], s)
               + jnp.einsum('bhij,bhjv->bhiv', qk_i, v_new))
        g_last = g_i[..., -1]
        k_dec = k_i * jnp.exp(g_last[..., None] - g_i)[..., None]
        s = s * jnp.exp(g_last)[..., None, None] + jnp.einsum('bhck,bhcv->bhkv', k_dec, v_new)
        return s, o_i

    s_final, o = lax.scan(step, s0, (qc, kc, u, w, gcum, qk))
    o = jnp.moveaxis(o, (0, 2), (1, 3)).reshape(bsz, n_chunks * chunk, n_h, -1)[:, :t_len]
    return o, s_final


def gated_deltanet(h, conv_buf, s0, w_qkvz, w_ba, conv_w, a_log, dt_bias, norm_w, w_out):
    f32 = jnp.float32
    bsz, t_len, _ = h.shape
    qkvz = h @ w_qkvz
    qkv, z = qkvz[..., :GDN_CONV_DIM], qkvz[..., GDN_CONV_DIM:]
    ba = (h @ w_ba).astype(f32)
    b, a = ba[..., :GDN_V_HEADS], ba[..., GDN_V_HEADS:]
    qkv, new_buf = causal_depthwise_conv(qkv, conv_buf, conv_w)
    qkv = jax.nn.silu(qkv).astype(f32)
    q = qkv[..., :GDN_KEY_DIM].reshape(bsz, t_len, GDN_K_HEADS, GDN_HEAD_K)
    k = qkv[..., GDN_KEY_DIM:2 * GDN_KEY_DIM].reshape(bsz, t_len, GDN_K_HEADS, GDN_HEAD_K)
    v = qkv[..., 2 * GDN_KEY_DIM:].reshape(bsz, t_len, GDN_V_HEADS, GDN_HEAD_V)
    q = l2_normalize(q) * (GDN_HEAD_K ** -0.5)
    k = l2_normalize(k)
    rep = GDN_V_HEADS // GDN_K_HEADS
    q = jnp.repeat(q, rep, axis=2)
    k = jnp.repeat(k, rep, axis=2)
    beta = jax.nn.sigmoid(b)
    g = -jnp.exp(a_log.astype(f32)) * jax.nn.softplus(a + dt_bias.astype(f32))
    o, s_new = gated_delta_rule(q, k, v, g, beta, s0.astype(f32))
    zh = z.reshape(bsz, t_len, GDN_V_HEADS, GDN_HEAD_V).astype(f32)
    o = rms_norm(o, norm_w) * jax.nn.silu(zh)
    out = o.reshape(bsz, t_len, GDN_VALUE_DIM).astype(h.dtype) @ w_out
    return out, new_buf, s_new.astype(s0.dtype)


def short_gated_conv(h, buf, w_in, conv_w, w_out):
    bcx = h @ w_in
    b_gate = bcx[..., :SC_DIM]
    c_gate = bcx[..., SC_DIM:2 * SC_DIM]
    xin = bcx[..., 2 * SC_DIM:]
    y, new_buf = causal_depthwise_conv(c_gate * xin, buf, conv_w)
    return (b_gate * y) @ w_out, new_buf


def squared_relu_mlp(h, w_up, w_down):
    return jnp.square(jax.nn.relu(h @ w_up)) @ w_down


def trunk(x, c, s_delta, s_qkv, s_sc, p):
    bsz = x.shape[0]
    new_delta, new_qkv, new_sc = [], [], []
    for i in range(DEPTH):
        mod = (jax.nn.silu(c) @ p['w_ada'][i] + p['b_ada'][i]).reshape(bsz, N_MOD, 1, D_MODEL)
        shift_m, scale_m, gate_m = mod[:, 0], mod[:, 1], mod[:, 2]
        shift_f, scale_f, gate_f = mod[:, 3], mod[:, 4], mod[:, 5]
        gains = p['norm_gain'][i]
        h = rms_norm(x, gains[0]) * (1.0 + scale_m) + shift_m
        j = i // N_MIXERS
        if i % N_MIXERS == 0:
            out, buf, s = gated_deltanet(h, s_qkv[j], s_delta[j], p['gdn_w_qkvz'][j], p['gdn_w_ba'][j],
                                         p['gdn_conv_w'][j], p['gdn_a_log'][j], p['gdn_dt_bias'][j],
                                         p['gdn_norm'][j], p['gdn_w_out'][j])
            new_qkv.append(buf)
            new_delta.append(s)
        else:
            out, buf = short_gated_conv(h, s_sc[j], p['sc_w_in'][j], p['sc_conv_w'][j], p['sc_w_out'][j])
            new_sc.append(buf)
        x = x + gate_m * rms_norm(out, gains[1])
        h = rms_norm(x, gains[2]) * (1.0 + scale_f) + shift_f
        x = x + gate_f * rms_norm(squared_relu_mlp(h, p['w_up'][i], p['w_down'][i]), gains[3])
    return x, jnp.stack(new_delta), jnp.stack(new_qkv), jnp.stack(new_sc)


def setup_inputs(seed: int = 0) -> dict:
    key = jax.random.key(seed)
    ks = jax.random.split(key, 24)
    f32 = jnp.float32
    D = D_MODEL

    def nrm(k, shape, scale):
        return jax.random.normal(k, shape, f32) * scale

    dt = jnp.exp(jax.random.uniform(ks[16], (N_A_LAYERS, GDN_V_HEADS), f32,
                                    math.log(1e-3), math.log(1e-1)))
    return {
        'x_prompt': nrm(ks[0], (BATCH, SEQ, D), 1.0),
        'x_sample': nrm(ks[1], (DEC_BATCH, DEC_SEQ, D), 1.0),
        'c_prompt': nrm(ks[2], (BATCH, D), 1.0),
        'c_sample': nrm(ks[3], (DEC_BATCH, D), 1.0),
        'state_delta': nrm(ks[4], (N_A_LAYERS, DEC_BATCH, GDN_V_HEADS, GDN_HEAD_K, GDN_HEAD_V), 0.1),
        'state_qkv_conv': nrm(ks[5], (N_A_LAYERS, DEC_BATCH, GDN_CONV_WIDTH - 1, GDN_CONV_DIM), 1.0),
        'state_short_conv': nrm(ks[6], (N_B_LAYERS, DEC_BATCH, SC_WIDTH - 1, SC_DIM), 1.0),
        'w_ada': nrm(ks[7], (DEPTH, D, N_MOD * D), 0.5 * D ** -0.5),
        'b_ada': nrm(ks[8], (DEPTH, N_MOD * D), 0.01),
        'norm_gain': 1.0 + nrm(ks[9], (DEPTH, 4, D), 0.02),
        'w_up': nrm(ks[10], (DEPTH, D, D_FF), D ** -0.5),
        'w_down': nrm(ks[11], (DEPTH, D_FF, D), D_FF ** -0.5),
        'gdn_w_qkvz': nrm(ks[12], (N_A_LAYERS, D, GDN_CONV_DIM + GDN_VALUE_DIM), D ** -0.5),
        'gdn_w_ba': nrm(ks[13], (N_A_LAYERS, D, 2 * GDN_V_HEADS), 0.1 * D ** -0.5),
        'gdn_conv_w': nrm(ks[14], (N_A_LAYERS, GDN_CONV_WIDTH, GDN_CONV_DIM), GDN_CONV_WIDTH ** -0.5),
        'gdn_a_log': jnp.log(jax.random.uniform(ks[15], (N_A_LAYERS, GDN_V_HEADS), f32, 1.0, 16.0)),
        'gdn_dt_bias': dt + jnp.log(-jnp.expm1(-dt)),
        'gdn_norm': 1.0 + nrm(ks[17], (N_A_LAYERS, GDN_HEAD_V), 0.02),
        'gdn_w_out': nrm(ks[18], (N_A_LAYERS, GDN_VALUE_DIM, D), GDN_VALUE_DIM ** -0.5),
        'sc_w_in': nrm(ks[19], (N_B_LAYERS, D, 3 * SC_DIM), D ** -0.5),
        'sc_conv_w': nrm(ks[20], (N_B_LAYERS, SC_WIDTH, SC_DIM), SC_WIDTH ** -0.5),
        'sc_w_out': nrm(ks[21], (N_B_LAYERS, SC_DIM, D), SC_DIM ** -0.5),
    }


def reference(x_prompt, x_sample, c_prompt, c_sample, state_delta, state_qkv_conv, state_short_conv,
              w_ada, b_ada, norm_gain, w_up, w_down, gdn_w_qkvz, gdn_w_ba, gdn_conv_w, gdn_a_log,
              gdn_dt_bias, gdn_norm, gdn_w_out, sc_w_in, sc_conv_w, sc_w_out):
    p = {'w_ada': w_ada, 'b_ada': b_ada, 'norm_gain': norm_gain, 'w_up': w_up, 'w_down': w_down,
         'gdn_w_qkvz': gdn_w_qkvz, 'gdn_w_ba': gdn_w_ba, 'gdn_conv_w': gdn_conv_w,
         'gdn_a_log': gdn_a_log, 'gdn_dt_bias': gdn_dt_bias, 'gdn_norm': gdn_norm,
         'gdn_w_out': gdn_w_out, 'sc_w_in': sc_w_in, 'sc_conv_w': sc_conv_w, 'sc_w_out': sc_w_out}
    bp = x_prompt.shape[0]
    zero_delta = jnp.zeros((state_delta.shape[0], bp) + state_delta.shape[2:], state_delta.dtype)
    zero_qkv = jnp.zeros((state_qkv_conv.shape[0], bp) + state_qkv_conv.shape[2:], state_qkv_conv.dtype)
    zero_sc = jnp.zeros((state_short_conv.shape[0], bp) + state_short_conv.shape[2:], state_short_conv.dtype)
    y_prompt, nd_p, nq_p, ns_p = trunk(x_prompt, c_prompt, zero_delta, zero_qkv, zero_sc, p)
    y_sample, nd_s, nq_s, ns_s = trunk(x_sample, c_sample, state_delta, state_qkv_conv, state_short_conv, p)
    return (y_prompt, y_sample, nd_p, nq_p, ns_p, nd_s, nq_s, ns_s)
```

```python
import contextlib
import numpy as np
import concourse.bass as bass
import concourse.mybir as mybir

F32 = mybir.dt.float32
BF16 = mybir.dt.bfloat16
ALU = mybir.AluOpType
AF = mybir.ActivationFunctionType
AX = mybir.AxisListType


class Dep:
    __slots__ = ("w", "r")

    def __init__(self):
        self.w = None
        self.r = {}


class Prod:
    def __init__(self, sem, inc, name):
        self.sem = sem
        self.inc = inc
        self.count = 0
        self.name = name
        self.is_dma = inc == 16


class Tile:
    def __init__(self, k, tensor, nparts=1, name=""):
        self.k = k
        self.t = tensor
        self.name = name
        self.deps = [Dep() for _ in range(nparts)]
        self.chan = None
        self.excl = False

    def __getitem__(self, idx):
        return self.t[idx]

    def p(self, *idx):
        if self.excl:
            return self
        return Part(self, idx)

    def dep_list(self):
        return self.deps

    def get_chan(self):
        if self.chan is None:
            self.chan = self.k.new_chan(self.name)
        return self.chan


class Part:
    def __init__(self, tile, idx):
        self.tile = tile
        self.idx = idx

    def dep_list(self):
        return [self.tile.deps[i] for i in self.idx]

    def get_chan(self):
        return self.tile.get_chan()


class K:
    ENGS = ("pe", "act", "dve", "pool", "sp")

    def __init__(self, nc, same_engine_sync=True):
        self.nc = nc
        self.stack = contextlib.ExitStack()
        self.q = {e: [] for e in self.ENGS}
        self.prod = {}
        self.seen = {e: {} for e in self.ENGS}
        self.same_engine_sync = same_engine_sync
        self.chans = []
        self.chan_by_name = {}
        self.nins = 0
        for e in ("pe", "act", "dve", "pool"):
            sem = self.stack.enter_context(nc.semaphore("s_" + e))
            self.prod[e] = Prod(sem, 1, e)

    def new_chan(self, name):
        if name in self.chan_by_name:
            return self.chan_by_name[name]
        sem = self.stack.enter_context(self.nc.semaphore("d%d_%s" % (len(self.chans), name)))
        c = Prod(sem, 16, "dma_" + name)
        self.chans.append(c)
        self.chan_by_name[name] = c
        return c

    def sbuf(self, name, shape, dtype, nparts=1):
        t = self.stack.enter_context(self.nc.sbuf_tensor(name, list(shape), dtype))
        return Tile(self, t, nparts, name)

    def psum(self, name, shape, dtype, nparts=1):
        t = self.stack.enter_context(self.nc.psum_tensor(name, list(shape), dtype))
        tl = Tile(self, t, 1, name)
        tl.excl = True
        return tl

    def dram(self, name, shape, dtype, kind, nparts=1):
        t = self.nc.dram_tensor(name, list(shape), dtype, kind=kind)
        return Tile(self, t, nparts, name)

    def _collect(self, reads, writes):
        need = {}
        raw = {}

        def add(pc, dst):
            if pc is None:
                return
            p, c = pc
            if dst.get(p, 0) < c:
                dst[p] = c

        for t in reads:
            for d in t.dep_list():
                add(d.w, need)
                add(d.w, raw)
                if getattr(t, "excl", False):
                    for p, c in d.r.items():
                        add((p, c), need)
        for t in writes:
            for d in t.dep_list():
                add(d.w, need)
                for p, c in d.r.items():
                    add((p, c), need)
        self._raw = raw
        return need

    def _emit_waits(self, en, need, own):
        seen = self.seen[en]
        for p, c in need.items():
            if p is own and not p.is_dma:
                if en == "pe" or not self.same_engine_sync:
                    continue
                c = self._raw.get(p, 0)
                if c == 0:
                    continue
            if p.is_dma:
                c = max(c, p.count)
            if seen.get(p, 0) >= c:
                continue
            seen[p] = c
            self.q[en].append(("wait", p.sem, c))

    def _mark(self, reads, writes, p, c):
        for t in reads:
            for d in t.dep_list():
                if d.r.get(p, 0) < c:
                    d.r[p] = c
        for t in writes:
            for d in t.dep_list():
                d.w = (p, c)
                d.r = {}

    def op(self, en, fn, reads=(), writes=()):
        p = self.prod[en]
        need = self._collect(reads, writes)
        self._emit_waits(en, need, p)
        p.count += 1
        self.q[en].append(("ins", fn, p.sem, 1))
        self._mark(reads, writes, p, p.count)
        self.nins += 1

    def dma(self, qn, out_ap, in_ap, reads, writes, chan_owner, **kw):
        ch = chan_owner.get_chan()
        need = self._collect(reads, writes)
        self._emit_waits(qn, need, None)
        ch.count += 16
        self.q[qn].append(("ins", lambda e: e.dma_start(out=out_ap, in_=in_ap, **kw), ch.sem, 16))
        self._mark(reads, writes, ch, ch.count)
        self.nins += 1


    def barrier(self):
        allp = [self.prod[e] for e in ("pe", "act", "dve", "pool")] + self.chans
        for en in self.ENGS:
            seen = self.seen[en]
            for p in allp:
                if p is self.prod.get(en):
                    continue
                if p.count and seen.get(p, 0) < p.count:
                    seen[p] = p.count
                    self.q[en].append(("wait", p.sem, p.count))

    def mm(self, out, lhsT, rhs, start, stop, reads, writes):
        self.op("pe", lambda e, o=out, l=lhsT, r=rhs, a=start, b=stop: e.matmul(o, lhsT=l, rhs=r, start=a, stop=b), reads, writes)

    def tr(self, out, in_, ident, reads, writes):
        self.op("pe", lambda e, o=out, i=in_, d=ident: e.transpose(o, i, d), reads, writes)

    def act(self, out, in_, func, reads, writes, **kw):
        self.op("act", lambda e, o=out, i=in_, f=func, kw=kw: e.activation(out=o, in_=i, func=f, **kw), reads, writes)

    def copy(self, en, out, in_, reads, writes):
        if en == "act":
            self.op("act", lambda e, o=out, i=in_: e.copy(out=o, in_=i), reads, writes)
        else:
            self.op(en, lambda e, o=out, i=in_: e.tensor_copy(out=o, in_=i), reads, writes)

    def tt(self, en, out, in0, in1, op, reads, writes):
        self.op(en, lambda e, o=out, a=in0, b=in1, p=op: e.tensor_tensor(out=o, in0=a, in1=b, op=p), reads, writes)

    def ts(self, en, out, in0, s1, s2, op0, op1, reads, writes):
        if s2 is None:
            s2, op1 = 0.0, ALU.add
        self.op(en, lambda e, o=out, a=in0, x=s1, y=s2, p0=op0, p1=op1: e.tensor_scalar(out=o, in0=a, scalar1=x, scalar2=y, op0=p0, op1=p1), reads, writes)

    def stt(self, out, in0, scalar, in1, op0, op1, reads, writes):
        self.op("dve", lambda e, o=out, a=in0, s=scalar, b=in1, p0=op0, p1=op1: e.scalar_tensor_tensor(out=o, in0=a, scalar=s, in1=b, op0=p0, op1=p1), reads, writes)

    def recip(self, out, in_, reads, writes):
        self.op("dve", lambda e, o=out, i=in_: e.reciprocal(out=o, in_=i), reads, writes)

    def memset(self, en, ap, val, writes):
        self.op(en, lambda e, a=ap, v=val: e.memset(a, v), [], writes)


    def finish(self):
        for ch in self.chans:
            if ch.count:
                self.q["sp"].append(("wait", ch.sem, ch.count))
        for e in ("pe", "act", "dve", "pool"):
            p = self.prod[e]
            if p.count:
                self.q["sp"].append(("wait", p.sem, p.count))
        nc = self.nc
        q = self.q

        def run(lst, eng):
            for it in lst:
                if it[0] == "wait":
                    eng.wait_ge(it[1], it[2])
                else:
                    ins = it[1](eng)
                    ins.then_inc(it[2], it[3])

        with nc.Block() as block:
            @block.sync
            def _(e):
                run(q["sp"], e)

            @block.scalar
            def _(e):
                run(q["act"], e)

            @block.vector
            def _(e):
                run(q["dve"], e)

            @block.gpsimd
            def _(e):
                run(q["pool"], e)

            @block.tensor
            def _(e):
                run(q["pe"], e)
        self.stack.close()


class Region:
    def __init__(self, k, name, nwords):
        self.k = k
        self.t = k.stack.enter_context(k.nc.sbuf_tensor(name, [128, nwords], F32))
        self.n = nwords
        self.off = 0
        self.name = name

    def reset(self):
        self.off = 0

    def alloc(self, name, shape, dtype, nparts=1):
        free = 1
        for d in shape[1:]:
            free *= d
        words = free if dtype == F32 else (free + 1) // 2
        assert self.off + words <= self.n, (self.name, name, self.off, words, self.n)
        ap = self.t[0:shape[0], self.off:self.off + words]
        self.off += words
        if dtype == BF16:
            ap = ap.bitcast(BF16)
        if len(shape) == 3:
            ap = ap.rearrange("p (a b) -> p a b", b=shape[2])
        elif len(shape) == 4:
            ap = ap.rearrange("p (a b c) -> p a b c", b=shape[2], c=shape[3])
        return Tile(self.k, ap, nparts, name)
from concourse.bass_utils import run_bass_kernel_spmd
D = 2048
NH = 32
NKH = 16
DFF = 8192
CONVD = 8192
EPS = 1e-6
NW = 3


class Gen:
    def __init__(self, NP, NS, DEPTH, stop_after=None):
        self.NP, self.NS, self.DEPTH = NP, NS, DEPTH
        self.NA, self.NB = (DEPTH + 1) // 2, DEPTH // 2
        NA, NB = self.NA, self.NB
        nc = bass.Bass("TRN2", target_bir_lowering=False)
        self.nc = nc
        k = K(nc)
        self.k = k

        def di(n, s, parts=1):
            return k.dram(n, s, F32, "ExternalInput", parts)

        def do(n, s, parts=1):
            return k.dram(n, s, F32, "ExternalOutput", parts)

        self.xp = di("xp", [NP, D])
        self.xs = di("xs", [NS, D])
        self.cv = di("cv", [NS + 1, D])
        self.sdel = di("sdel", [NA, NS, NH, 128, 128])
        self.sqkv = di("sqkv", [NA, NS * 3, CONVD])
        self.ssc = di("ssc", [max(NB, 1), NS * 2, D])
        self.w_ada = di("w_ada", [DEPTH, D, 6 * D])
        self.b_ada = di("b_ada", [DEPTH, 6 * D])
        self.norm_gain = di("norm_gain", [DEPTH, 4, D])
        self.w_up = di("w_up", [DEPTH, D, DFF])
        self.w_down = di("w_down", [DEPTH, DFF, D])
        self.w_qkvz = di("gdn_w_qkvz", [NA, D, 12288])
        self.w_ba = di("gdn_w_ba", [NA, D, 64])
        self.g_conv_w = di("gdn_conv_w", [NA, 4, CONVD])
        self.gsm = di("gsm", [NA, 1024])
        self.g_w_out = di("gdn_w_out", [NA, 4096, D])
        self.sc_w_in = di("sc_w_in", [max(NB, 1), D, 3 * D])
        self.sc_conv_w = di("sc_conv_w", [max(NB, 1), 3, D])
        self.sc_w_out = di("sc_w_out", [max(NB, 1), D, D])
        self.cst_d = di("cst", [128, 6 * 128])

        self.yp = do("yp", [NP, D])
        self.ys = do("ys", [NS, D])
        self.ndp = do("ndp", [NA, NH, 128, 128])
        self.nqp = do("nqp", [NA, 3, CONVD])
        self.nsp = do("nsp", [max(NB, 1), 2, D])
        self.nds = do("nds", [NA, NS, NH, 128, 128])
        self.nqs = do("nqs", [NA, NS * 3, CONVD])
        self.nss = do("nss", [max(NB, 1), NS * 2, D])
        self.xscr = k.dram("xscr", [NP + NS, D], F32, "Internal", NP // 128 + 1)
        self.modscr = k.dram("modscr", [DEPTH, NS + 1, 6 * D], F32, "Internal", DEPTH)
        self.NPIECE = NA * 65 + DEPTH * 64 + NB * 32 + 4
        self.wscr_l = [k.dram("wscr%d" % i, [200, 128, 4096], BF16, "Internal", 200) for i in range((self.NPIECE + 199) // 200)]
        self.wkeys = {}

        class _Own:
            def __init__(self, kk, name):
                self.kk, self.name, self.chan = kk, name, None

            def get_chan(self):
                if self.chan is None:
                    self.chan = self.kk.new_chan(self.name)
                return self.chan
        self.wconv = [_Own(k, "wconv%d" % i) for i in range(8)]

        self.cst = k.sbuf("cst_sb", [128, 6 * 128], F32)
        self.identb = k.sbuf("identb", [128, 128], BF16)
        self.S = k.sbuf("S_state", [128, NH, 128], F32, NH)
        self.qtail = k.sbuf("qtail", [128, 64, 3], F32, 64)
        self.sctail = k.sbuf("sctail", [128, 16, 2], F32, 16)
        self.cw = k.sbuf("cw", [128, 64, 4], F32)
        self.scw = k.sbuf("scw", [128, 16, 3], F32)
        self.small = k.sbuf("small", [128, 64], F32, 8)
        self.lay = k.sbuf("lay", [128, 2 * NH + 128], F32)
        self.tok = k.sbuf("tok", [128, 12, 4, NH], F32, 12)
        self.hT = k.sbuf("hT", [128, 16, 512], BF16, 4)
        self.wp = [k.sbuf("wp%d" % i, [128, 16, 256], BF16) for i in range(NW)]
        self.wi = 0
        self.RX = Region(k, "RX", 9216)
        self.RO = Region(k, "RO", 8192)
        self.RA = Region(k, "RA", 8192)
        self.MM = [k.psum("mm%d" % i, [128, 512], F32, 4) for i in range(3)]
        self.mmi = 0
        self.TPB = [k.psum("tpb%d" % i, [128, 1024], BF16, 8) for i in range(2)]
        self.tpi = 0
        self.DR = [k.psum("dr%d" % i, [128, 512], F32, 4) for i in range(3)]
        self.dri = 0

        c = self.cst
        self.ident = c[:, 0:128]
        self.negLs = c[:, 128:256]
        self.negUs = c[:, 256:384]
        self.Ui = c[:, 384:512]
        self.SelLast = c[:, 512:640]
        self.ones = c[:, 640:768]
        k.dma("sp", c[:], self.cst_d[:], [], [c], c)
        k.copy("dve", self.identb[:], self.ident, [c], [self.identb])
        k.memset("dve", self.small[:, 0:1], EPS, [self.small.p(0)])
        self.epsc = self.small[:, 0:1]

        self.stop_after = stop_after
        self.stage = 0
        self.dbg = {}
        try:
            self.build(stop_after)
        except StopIteration:
            pass
        k.finish()

    def mm_bank(self):
        b = self.MM[self.mmi % 3]
        self.mmi += 1
        return b

    def dr_slot(self):
        i = self.dri % 12
        self.dri += 1
        t = self.DR[i % 3]
        q = i // 3
        return t.p(q), t[:, q * 128:(q + 1) * 128]

    def tp_slot(self):
        i = self.tpi % 16
        self.tpi += 1
        t = self.TPB[i % 2]
        q = i // 2
        return t.p(q), t[:, q * 128:(q + 1) * 128]

    def piece(self, specs):
        k = self.k
        key = tuple((wt.name, l, r0, nk, c0, nc_) for (wt, l, r0, nk, c0, nc_) in specs)
        w = self.wp[self.wi % NW]
        self.wi += 1
        nk = specs[0][3]
        tot = sum(sp_[5] for sp_ in specs)
        if key not in self.wkeys:
            idx = len(self.wkeys)
            assert idx < self.NPIECE
            self.wkeys[key] = idx
            wsc = self.wscr_l[idx // 200]
            dst = wsc[idx % 200][:, 0:nk * tot].rearrange("p (kc n) -> p kc n", n=tot)
            col = 0
            own = self.wconv[idx % 8]
            ch = own.get_chan()
            if ch.count and k.seen["pool"].get(ch, 0) < ch.count:
                k.seen["pool"][ch] = ch.count
                k.q["pool"].append(("wait", ch.sem, ch.count))
            for (wt, l, r0, nk_, c0, nc_) in specs:
                k.dma("pool", dst[:, :, col:col + nc_], self.wview(wt, l, r0, nk_, c0, nc_), [], [wsc.p(idx % 200)], own)
                col += nc_
        idx = self.wkeys[key]
        wsc = self.wscr_l[idx // 200]
        src = wsc[idx % 200][:, 0:nk * tot].rearrange("p (kc n) -> p kc n", n=tot)
        k.dma("sp", w[:, 0:nk, 0:tot], src, [wsc.p(idx % 200)], [w], w)
        return w

    def wspec(self, wt, l, r0, nk, c0, ncols):
        return (wt, l, r0, nk, c0, ncols)

    def wview(self, wt, l, r0, nk, c0, ncols):
        return wt[l, r0:r0 + nk * 128, c0:c0 + ncols].rearrange("(kc p) n -> p kc n", p=128)

    def rstd_col(self, ss, scale):
        k = self.k
        R = ss.shape[0]
        k.act(ss, ss, AF.Sqrt, [self.small], [self.small.p(1)], bias=self.epsc[0:R, :], scale=scale)
        k.recip(ss, ss, [self.small.p(1)], [self.small.p(1)])

    def adaln(self):
        k = self.k
        NS = self.NS
        R = NS + 1
        self.RO.reset()
        self.RX.reset()
        c17 = self.RX.alloc("c17", [R, D], F32)
        scT = self.RX.alloc("scT", [128, 16, R], F32)
        wA = [self.RO.alloc("wA%d" % i, [128, 16, 256], F32) for i in range(2)]
        bias = [self.RX.alloc("bias%d" % i, [1, 256], F32) for i in range(2)]
        msb = [self.RX.alloc("msb%d" % i, [R, 256], F32) for i in range(2)]
        k.dma("sp", c17[:], self.cv[:], [], [c17], c17)
        k.act(c17[:], c17[:], AF.Silu, [c17], [c17])
        for kc in range(16):
            dp, ps = self.dr_slot()
            k.tr(ps[:, 0:R], c17[0:R, kc * 128:(kc + 1) * 128], self.ident[0:R, 0:R], [c17, self.cst], [dp])
            k.copy("dve", scT[:, kc, :], ps[:, 0:R], [dp], [scT])
        it = 0
        for l in range(self.DEPTH):
            for blk in range(48):
                w = wA[it % 2]
                b = bias[it % 2]
                m = msb[it % 2]
                it += 1
                k.dma("sp", w[:], self.wview(self.w_ada, l, 0, 16, blk * 256, 256), [], [w], w)
                k.dma("sp", b[:], self.b_ada[l:l + 1, blk * 256:(blk + 1) * 256], [], [b], b)
                bank = self.mm_bank()
                for kc in range(16):
                    k.mm(bank[0:R, 0:256], scT[:, kc, :], w[:, kc, :], kc == 0, False, [scT, w], [bank])
                k.mm(bank[0:R, 0:256], self.ones[0:1, 0:R], b[0:1, :], False, True, [self.cst, b], [bank])
                k.copy("act", m[:], bank[0:R, 0:256], [bank], [m])
                k.dma("sp", self.modscr[l, :, blk * 256:(blk + 1) * 256], m[:], [m], [self.modscr.p(l)], m)
        k.barrier()

    def alloc_x(self):
        self.RX.reset()
        self.XT = [self.RX.alloc("xt%d" % i, [128, D], F32) for i in range(2)]
        self.MOD1 = self.RX.alloc("mod1", [128, D], F32)
        self.MOD2 = self.RX.alloc("mod2", [128, D], F32)
        self.HB = self.RX.alloc("hb", [128, D], BF16)
        self.xi = 0

    def modrow(self, l, idx, R):
        if R == 128:
            return self.modscr[l, self.NS, idx * D:(idx + 1) * D].partition_broadcast(128)
        return self.modscr[l, 0:R, idx * D:(idx + 1) * D]

    def stageA(self, l, which, xrows, nsub, R):
        k = self.k
        self.alloc_x()
        M1, M2 = self.MOD1, self.MOD2
        g = self.XT[1]
        k.dma("sp", M1[0:R, :], self.modrow(l, 1 + 3 * which, R), [self.modscr.p(l)], [M1], M1)
        k.dma("sp", M2[0:R, :], self.modrow(l, 0 + 3 * which, R), [self.modscr.p(l)], [M2], M2)
        k.dma("sp", g[0:R, :], self.norm_gain[l, 2 * which, :].partition_broadcast(R), [], [g], g)
        k.stt(M1[0:R, :], M1[0:R, :], 1.0, g[0:R, :], ALU.add, ALU.mult, [M1, g], [M1])
        self.cut(1)
        ss = self.small[0:R, 8:8 + nsub]
        k.memset("dve", ss, 0.0, [self.small.p(1)])
        for s in range(nsub):
            xt = self.XT[self.xi % 2]
            self.xi += 1
            part, ap = xrows(s)
            k.dma("sp", xt[0:R, :], ap, [part], [xt], xt)
            ssc = self.small[0:R, 8 + s:9 + s]
            k.act(self.HB[0:R, :], xt[0:R, :], AF.Square, [xt], [self.HB, self.small.p(1)], accum_out=ssc)
            self.cut(2)
            self.rstd_col(ssc, 1.0 / D)
            self.cut(3)
            k.stt(xt[0:R, :], xt[0:R, :], ssc, M1[0:R, :], ALU.mult, ALU.mult, [xt, self.small.p(1), M1], [xt])
            k.tt("dve", self.HB[0:R, :], xt[0:R, :], M2[0:R, :], ALU.add, [xt, M2], [self.HB])
            self.cut(4)
            for half in range(2):
                tp = self.TPB[self.tpi % 2]
                self.tpi += 1
                for j in range(8):
                    kc = half * 8 + j
                    k.tr(tp[:, j * 128:j * 128 + R], self.HB[0:R, kc * 128:(kc + 1) * 128], self.identb[0:R, 0:R],
                         [self.HB, self.identb], [tp])
                self.cut(5)
                src = tp[:, :].rearrange("p (a b) -> p a b", b=128)[:, :, 0:R]
                k.copy("act", self.hT[:, half * 8:half * 8 + 8, s * 128:s * 128 + R], src, [tp], [self.hT.p(s)])

    def stageC(self, l, which, xsrc, xdst, nsub, R, outbuf):
        k = self.k
        self.alloc_x()
        M1, M2 = self.MOD1, self.MOD2
        k.dma("sp", M1[0:R, :], self.modrow(l, 2 + 3 * which, R), [self.modscr.p(l)], [M1], M1)
        k.dma("sp", M2[0:R, :], self.norm_gain[l, 1 + 2 * which, :].partition_broadcast(R), [], [M2], M2)
        k.tt("dve", M1[0:R, :], M1[0:R, :], M2[0:R, :], ALU.mult, [M1, M2], [M1])
        if R == 128:
            self.dump("gate", M1, M1[:, :], [128, D])
        ss = self.small[0:R, 8:8 + nsub]
        k.memset("dve", ss, 0.0, [self.small.p(1)])
        for s in range(nsub):
            xt = self.XT[self.xi % 2]
            self.xi += 1
            ob = outbuf[0:R, s, :]
            part, ap = xsrc(s)
            k.dma("sp", xt[0:R, :], ap, [part], [xt], xt)
            ssc = self.small[0:R, 8 + s:9 + s]
            k.act(self.HB[0:R, :], ob, AF.Square, [outbuf.p(s)], [self.HB, self.small.p(1)], accum_out=ssc)
            self.rstd_col(ssc, 1.0 / D)
            if R == 128 and s == 3:
                self.dump("rs", self.small, self.small[:, 8:12], [128, 4])
            k.stt(ob, ob, ssc, M1[0:R, :], ALU.mult, ALU.mult, [outbuf.p(s), self.small.p(1), M1], [outbuf.p(s)])
            k.tt("pool", xt[0:R, :], xt[0:R, :], ob, ALU.add, [xt, outbuf.p(s)], [xt])
            dpart, dap = xdst(s)
            k.dma("sp", dap, xt[0:R, :], [xt], [dpart], xt)

    def out_proj(self, wt, l, nk_total, actT, nsub, R, outbuf, accumulate=False, row0=0):
        k = self.k
        npieces = nk_total // 16
        for cb in range(D // 256):
            banks = [self.mm_bank() for _ in range((nsub + 1) // 2)]
            assert npieces <= NW - 1
            ws = [self.piece([self.wspec(wt, l, row0 + pi * 2048, 16, cb * 256, 256)]) for pi in range(npieces)]
            for s in range(nsub):
                bank = banks[s // 2]
                c0 = (s % 2) * 256
                for pi in range(npieces):
                    w = ws[pi]
                    for kc in range(16):
                        kg = pi * 16 + kc
                        k.mm(bank[0:R, c0:c0 + 256], actT[:, kg, s * 128:s * 128 + R], w[:, kc, :],
                             pi == 0 and kc == 0, pi == npieces - 1 and kc == 15,
                             [w, actT], [bank])
            for s in range(nsub):
                bank = banks[s // 2]
                c0 = (s % 2) * 256
                dst = outbuf[0:R, s, cb * 256:(cb + 1) * 256]
                if accumulate:
                    k.tt("dve", dst, dst, bank[0:R, c0:c0 + 256], ALU.add,
                         [bank.p(2 * (s % 2), 2 * (s % 2) + 1), outbuf.p(s)], [outbuf.p(s)])
                else:
                    k.copy("act", dst, bank[0:R, c0:c0 + 256], [bank.p(2 * (s % 2), 2 * (s % 2) + 1)], [outbuf.p(s)])

    def fm_chunks(self, w, nchunk, nk, ncols, consume):
        k = self.k
        for j in range(nchunk):
            bank = self.mm_bank()
            for kc in range(nk):
                k.mm(bank[:, 0:ncols], w[:, kc, j * 128:(j + 1) * 128], self.hT[:, kc, 0:ncols], kc == 0, kc == nk - 1,
                     [w, self.hT], [bank])
            consume(j, bank)

    def mlp(self, l, nsub, R, ncols):
        k = self.k
        self.RO.reset()
        self.RA.reset()
        outbuf = self.RO.alloc("outbuf", [128, 4, D], F32, 4)
        uT = self.RA.alloc("uT", [128, 32, 512], BF16, 32)
        self.RX.reset()
        tmp = [self.RX.alloc("rl%d" % i, [128, 512], F32) for i in range(2)]
        ti = 0
        for hf in range(2):
            for pc in range(16):
                c0 = hf * 4096 + pc * 256
                w = self.piece([self.wspec(self.w_up, l, 0, 16, c0, 256)])

                def consume(j, bank, pc=pc):
                    nonlocal ti
                    t = tmp[ti % 2]
                    ti += 1
                    k.act(t[:, 0:ncols], bank[:, 0:ncols], AF.Relu, [bank], [t])
                    k.tt("dve", uT[:, pc * 2 + j, 0:ncols], t[:, 0:ncols], t[:, 0:ncols], ALU.mult, [t], [uT.p(pc * 2 + j)])
                self.fm_chunks(w, 2, 16, ncols, consume)
            self.out_proj(self.w_down, l, 32, uT, nsub, R, outbuf, accumulate=(hf == 1), row0=hf * 4096)
        k.barrier()
        return outbuf

    def sc_mixer(self, l, j_layer, nsub, R, ncols, sample, first, last):
        k = self.k
        jl = j_layer
        self.RO.reset()
        self.RA.reset()
        self.RX.reset()
        outbuf = self.RO.alloc("outbuf", [128, 4, D], F32, 4)
        gT = self.RA.alloc("gT", [128, 16, 512], BF16, 16)
        csb = [self.RX.alloc("csb%d" % i, [128, 512], F32) for i in range(2)]
        uraw = [self.RX.alloc("uraw%d" % i, [128, 514], F32) for i in range(2)]
        acc = [self.RX.alloc("acc%d" % i, [128, 512], F32) for i in range(2)]
        if sample:
            NS = self.NS
            bufsb = self.RX.alloc("bufsb", [2 * NS, D], F32)
            bufT = self.RX.alloc("bufT", [128, 16, 2 * NS], F32)
            unT = self.RX.alloc("unT", [128, 16, NS], F32, 16)
            unew = self.RX.alloc("unew", [NS, D], F32)
            k.dma("sp", bufsb[:], self.ssc[jl], [], [bufsb], bufsb)
            for c in range(16):
                dp, ps = self.dr_slot()
                k.tr(ps[:, 0:2 * NS], bufsb[:, c * 128:(c + 1) * 128], self.ident[0:2 * NS, 0:2 * NS], [bufsb, self.cst], [dp])
                k.copy("dve", bufT[:, c, :], ps[:, 0:2 * NS], [dp], [bufT])
        it = 0
        for jp in range(8):
            wb = self.piece([self.wspec(self.sc_w_in, jl, 0, 16, jp * 256, 256)])
            wc = self.piece([self.wspec(self.sc_w_in, jl, 0, 16, 2048 + jp * 256, 256)])
            wx = self.piece([self.wspec(self.sc_w_in, jl, 0, 16, 4096 + jp * 256, 256)])
            for jj in range(2):
                c = jp * 2 + jj
                cs, ur, ac = csb[it % 2], uraw[it % 2], acc[it % 2]
                it += 1
                bank = self.mm_bank()
                for kc in range(16):
                    k.mm(bank[:, 0:ncols], wc[:, kc, jj * 128:(jj + 1) * 128], self.hT[:, kc, 0:ncols], kc == 0, kc == 15, [wc, self.hT], [bank])
                k.copy("act", cs[:, 0:ncols], bank[:, 0:ncols], [bank], [cs])
                bank = self.mm_bank()
                for kc in range(16):
                    k.mm(bank[:, 0:ncols], wx[:, kc, jj * 128:(jj + 1) * 128], self.hT[:, kc, 0:ncols], kc == 0, kc == 15, [wx, self.hT], [bank])
                if not sample:
                    k.tt("dve", ur[:, 2:2 + ncols], cs[:, 0:ncols], bank[:, 0:ncols], ALU.mult, [cs, bank], [ur])
                    if first:
                        k.memset("pool", ur[:, 0:2], 0.0, [ur])
                    else:
                        k.copy("pool", ur[:, 0:2], self.sctail[:, c, :], [self.sctail.p(c)], [ur])
                    k.copy("pool", self.sctail[:, c, :], ur[:, ncols:ncols + 2], [ur], [self.sctail.p(c)])
                    k.ts("dve", ac[:, 0:ncols], ur[:, 0:ncols], self.scw[:, c, 0:1], None, ALU.mult, None, [ur, self.scw], [ac])
                    for t in (1, 2):
                        k.stt(ac[:, 0:ncols], ur[:, t:t + ncols], self.scw[:, c, t:t + 1], ac[:, 0:ncols], ALU.mult, ALU.add, [ur, self.scw, ac], [ac])
                else:
                    NS = self.NS
                    k.tt("dve", unT[:, c, :], cs[:, 0:ncols], bank[:, 0:ncols], ALU.mult, [cs, bank], [unT.p(c)])
                    bv = bufT[:, c, :].rearrange("p (b j) -> p b j", j=2)
                    k.ts("dve", ac[:, 0:NS], bv[:, :, 0], self.scw[:, c, 0:1], None, ALU.mult, None, [bufT, self.scw], [ac])
                    k.stt(ac[:, 0:NS], bv[:, :, 1], self.scw[:, c, 1:2], ac[:, 0:NS], ALU.mult, ALU.add, [bufT, self.scw, ac], [ac])
                    k.stt(ac[:, 0:NS], unT[:, c, :], self.scw[:, c, 2:3], ac[:, 0:NS], ALU.mult, ALU.add, [unT.p(c), self.scw, ac], [ac])
                bank = self.mm_bank()
                for kc in range(16):
                    k.mm(bank[:, 0:ncols], wb[:, kc, jj * 128:(jj + 1) * 128], self.hT[:, kc, 0:ncols], kc == 0, kc == 15, [wb, self.hT], [bank])
                k.tt("dve", gT[:, c, 0:ncols], ac[:, 0:ncols], bank[:, 0:ncols], ALU.mult, [ac, bank], [gT.p(c)])
        if sample:
            NS = self.NS
            for c in range(16):
                dp, ps = self.dr_slot()
                k.tr(ps[0:NS, :], unT[:, c, :], self.ident, [unT.p(c), self.cst], [dp])
                k.copy("act", unew[:, c * 128:(c + 1) * 128], ps[0:NS, :], [dp], [unew])
            dst = self.nss[jl].rearrange("(b j) d -> b j d", j=2)
            src = self.ssc[jl].rearrange("(b j) d -> b j d", j=2)
            k.dma("sp", dst[:, 1, :], unew[:], [unew], [self.nss], unew)
            k.dma("sp", dst[:, 0, :], src[:, 1, :], [], [self.nss], unew)
        elif last:
            tl = self.RX.alloc("tl", [2, D], F32)
            for c in range(16):
                dp, ps = self.dr_slot()
                k.tr(ps[0:2, :], self.sctail[:, c, :], self.ident, [self.sctail.p(c), self.cst], [dp])
                k.copy("act", tl[:, c * 128:(c + 1) * 128], ps[0:2, :], [dp], [tl])
            k.dma("sp", self.nsp[jl], tl[:], [tl], [self.nsp], tl)
        self.out_proj(self.sc_w_out, jl, 16, gT, nsub, R, outbuf)
        k.barrier()
        return outbuf

    def layer_consts(self, l):
        k = self.k
        self.RX.reset()
        if l % 2 == 0:
            jl = l // 2
            lay = self.lay
            k.dma("sp", lay[:, 0:192], self.gsm[jl, 0:192].partition_broadcast(128), [], [lay], lay)
            self.cut(9)
            cwr = self.RX.alloc("cwr", [32, CONVD], F32)
            k.memset("pool", cwr[:], 0.0, [cwr])
            k.dma("sp", cwr[0:4, :], self.g_conv_w[jl], [], [cwr], cwr)
            for c in range(64):
                dp, ps = self.dr_slot()
                k.tr(ps[:, 0:32], cwr[:, c * 128:(c + 1) * 128], self.ident[0:32, 0:32], [cwr, self.cst], [dp])
                k.copy("dve", self.cw[:, c, :], ps[:, 0:4], [dp], [self.cw])
            self.cut(10)
            lay = self.lay
            self.cut(12)
            k.act(lay[:, NH:2 * NH], lay[:, NH:2 * NH], AF.Exp, [lay], [lay])
            self.cut(13)
            k.ts("dve", lay[:, NH:2 * NH], lay[:, NH:2 * NH], -1.0, None, ALU.mult, None, [lay], [lay])
            self.cut(11)
        else:
            jl = l // 2
            cwr = self.RX.alloc("cwr", [32, D], F32)
            k.memset("pool", cwr[:], 0.0, [cwr])
            k.dma("sp", cwr[0:3, :], self.sc_conv_w[jl], [], [cwr], cwr)
            for c in range(16):
                dp, ps = self.dr_slot()
                k.tr(ps[:, 0:32], cwr[:, c * 128:(c + 1) * 128], self.ident[0:32, 0:32], [cwr, self.cst], [dp])
                k.copy("dve", self.scw[:, c, :], ps[:, 0:3], [dp], [self.scw])
        k.barrier()

    def xrows_fn(self, src_tile, row0, R, part_fn):
        def f(s):
            return part_fn(s), src_tile[row0 + s * 128:row0 + s * 128 + R, :]
        return f

    def dump(self, name, tile, ap, shape):
        import os
        if not os.environ.get("DUMP"):
            return
        if name in self.dbg:
            return
        d = self.k.dram("dbg_" + name, list(shape), F32, "ExternalOutput")
        self.dbg[name] = d
        self.k.dma("pool", d[:], ap, [tile], [d], tile)

    def cut(self, n):
        import os
        if os.environ.get("CUT") == str(n):
            self.k.barrier()
            raise StopIteration

    def ck(self):
        self.stage += 1
        if self.stop_after is not None and self.stage >= self.stop_after:
            self.k.barrier()
            raise StopIteration

    def build(self, stop_after):
        k = self.k
        NP, NS, DEPTH = self.NP, self.NS, self.DEPTH
        self.adaln()
        self.ck()
        import os
        nomlp = bool(os.environ.get("NOMLP"))
        subs = [(l_, w_) for l_ in range(DEPTH) for w_ in range(2) if not (nomlp and w_ == 1)]
        ntile = NP // 512
        for l in range(DEPTH):
            self.layer_consts(l)
            for which in range(2):
                if (l, which) not in subs:
                    continue
                is_first = (l, which) == subs[0]
                is_last = (l, which) == subs[-1]
                for ps_ in range(ntile + 1):
                    sample = ps_ == ntile
                    if sample:
                        nsub, R, ncols, row0 = 1, NS, NS, NP
                        src_in, dst_out = self.xs, self.ys
                        in0, out0 = 0, 0
                    else:
                        nsub, R, ncols, row0 = 4, 128, 512, ps_ * 512
                        src_in, dst_out = self.xp, self.yp
                        in0, out0 = row0, row0
                    sp0 = row0 // 128

                    def scr_part(s, sp0=sp0):
                        return self.xscr.p(sp0 + s)

                    def io_part(t):
                        return lambda s: t
                    if is_first:
                        xsrc = self.xrows_fn(src_in, in0, R, io_part(src_in))
                    else:
                        xsrc = self.xrows_fn(self.xscr, row0, R, scr_part)
                    if is_last:
                        xdst = self.xrows_fn(dst_out, out0, R, io_part(dst_out))
                    else:
                        xdst = self.xrows_fn(self.xscr, row0, R, scr_part)
                    self.stageA(l, which, xsrc, nsub, R)
                    k.barrier()
                    self.ck()
                    if which == 1:
                        outbuf = self.mlp(l, nsub, R, ncols)
                    elif l % 2 == 1:
                        outbuf = self.sc_mixer(l, l // 2, nsub, R, ncols, sample, ps_ == 0, ps_ == ntile - 1)
                    else:
                        outbuf = self.gdn_mixer(l, l // 2, nsub, R, ncols, sample, ps_ == 0, ps_ == ntile - 1)
                    self.ck()
                    self.stageC(l, which, xsrc, xdst, nsub, R, outbuf)
                    k.barrier()
                    self.ck()

    def gdn_mixer(self, l, jl, nsub, R, ncols, sample, first, last):
        k = self.k
        self.RO.reset()
        self.RA.reset()
        self.RX.reset()
        if sample:
            return self.gdn_sample(l, jl)
        NHh = NH
        oT = self.RA.alloc("oT", [128, 32, 512], BF16, 32)
        tok = self.tok
        lay = self.lay
        if first:
            k.memset("pool", self.S[:, :, :], 0.0, [self.S])
        w = self.piece([self.wspec(self.w_ba, jl, 0, 16, 0, 64)])
        bank = self.mm_bank()
        for s in range(4):
            for kc in range(16):
                k.mm(bank[:, s * 64:(s + 1) * 64], self.hT[:, kc, s * 128:(s + 1) * 128], w[:, kc, 0:64], kc == 0, kc == 15,
                     [w, self.hT], [bank])
        bview = bank[:, 0:256].rearrange("p (s x) -> p s x", x=64)
        T = lambda i: tok[:, i]
        TP = lambda i: tok.p(i)
        k.act(T(0), bview[:, :, 0:32], AF.Exp, [bank], [TP(0)], scale=-1.0)
        k.act(T(0), T(0), AF.Ln, [TP(0)], [TP(0)], bias=1.0)
        k.ts("dve", T(0), T(0), -0.5, None, ALU.mult, None, [TP(0)], [TP(0)])
        for s in range(4):
            k.tt("dve", tok[:, 1, s, :], bview[:, s, 32:64], lay[:, 0:NHh], ALU.add, [bank, lay], [TP(1)])
        k.act(T(1), T(1), AF.Exp, [TP(1)], [TP(1)])
        k.act(T(1), T(1), AF.Ln, [TP(1)], [TP(1)], bias=1.0)
        for s in range(4):
            k.tt("dve", tok[:, 1, s, :], tok[:, 1, s, :], lay[:, NHh:2 * NHh], ALU.mult, [TP(1), lay], [TP(1)])
        for s in range(4):
            dp, ps = self.dr_slot()
            k.mm(ps[:, 0:32], self.Ui, tok[:, 1, s, :], True, True, [self.cst, TP(1)], [dp])
            k.copy("dve", tok[:, 2, s, :], ps[:, 0:32], [dp], [TP(2)])
        for s in range(4):
            dp, ps = self.dr_slot()
            k.mm(ps[:, 0:32], self.SelLast, tok[:, 2, s, :], True, True, [self.cst, TP(2)], [dp])
            k.copy("dve", tok[:, 4, s, :], ps[:, 0:32], [dp], [TP(4)])
        k.ts("dve", T(3), T(2), -1.0, None, ALU.mult, None, [TP(2)], [TP(3)])
        k.act(T(5), T(2), AF.Exp, [TP(2)], [TP(5)])
        k.tt("dve", T(10), T(0), T(2), ALU.add, [TP(0), TP(2)], [TP(10)])
        k.act(T(6), T(10), AF.Exp, [TP(10)], [TP(6)])
        k.ts("dve", T(6), T(6), -1.0, None, ALU.mult, None, [TP(6)], [TP(6)])
        k.act(T(7), T(0), AF.Exp, [TP(0)], [TP(7)])
        k.tt("dve", T(10), T(4), T(3), ALU.add, [TP(4), TP(3), TP(10)], [TP(10)])
        k.tt("dve", T(10), T(10), T(0), ALU.add, [TP(10), TP(0)], [TP(10)])
        k.act(T(8), T(10), AF.Exp, [TP(10)], [TP(8)])
        k.act(T(9), T(4), AF.Exp, [TP(4)], [TP(9)])

        RO, RX = self.RO, self.RX
        raw = RO.alloc("raw", [128, 4, 515], F32, 4)
        cvb = [RO.alloc("cvb%d" % i, [128, 512], F32) for i in range(2)]
        sq = RO.alloc("sq", [128, 512], BF16)
        rinv = RO.alloc("rinv", [128, 512], F32)
        QnT = RO.alloc("QnT", [128, 512], BF16)
        KnT = RO.alloc("KnT", [128, 512], BF16)
        vT = RO.alloc("vT", [128, 2, 512], BF16, 2)
        Vs = RO.alloc("Vs", [128, 4, 2, 128], F32, 8)
        zg = RO.alloc("zg", [128, 4, 256], F32, 4)
        Kd = RO.alloc("Kd", [128, 4, 2, 128], BF16, 8)
        XZ = [[RX.alloc("xz%d_%d" % (u, i), [128, 128], F32) for i in range(4)] for u in range(4)]
        PP = [[RX.alloc("pp%d_%d" % (u, i), [128, 128], F32) for i in range(2)] for u in range(4)]
        TT_ = [[RX.alloc("tt%d_%d" % (u, i), [128, 128], F32) for i in range(3)] for u in range(4)]
        DG = [RX.alloc("dg%d" % u, [128, 256], F32) for u in range(4)]
        PT = [[RX.alloc("pt%d_%d" % (j, c), [128, 128], BF16) for c in range(4)] for j in range(2)]
        QK = [[RX.alloc("qk%d_%d" % (j, c), [128, 128], BF16) for c in range(4)] for j in range(2)]
        Rb = [RX.alloc("rb%d" % j, [128, 128], BF16) for j in range(2)]
        Yb = [RX.alloc("yb%d" % j, [128, 128], BF16) for j in range(2)]
        ogb = [RX.alloc("og%d" % j, [128, 128], BF16) for j in range(2)]
        Of = [RX.alloc("of%d" % j, [128, 128], F32) for j in range(2)]
        tO = [RX.alloc("to%d" % j, [128, 128], F32) for j in range(2)]
        Sbf = RX.alloc("sbf", [128, 2, 128], BF16, 2)
        onesb = RX.alloc("onesb", [128, 128], BF16)
        k.copy("dve", onesb[:], self.ones, [self.cst], [onesb])
        ci = 0
        for hk in range(NKH):
            wqk = self.piece([self.wspec(self.w_qkvz, jl, 0, 16, hk * 128, 128),
                              self.wspec(self.w_qkvz, jl, 0, 16, 2048 + hk * 128, 128)])
            wv = self.piece([self.wspec(self.w_qkvz, jl, 0, 16, 4096 + hk * 256, 256)])
            wz = self.piece([self.wspec(self.w_qkvz, jl, 0, 16, 8192 + hk * 256, 256)])
            for j in range(2):
                k.copy("act", Sbf[:, j, :], self.S[:, 2 * hk + j, :], [self.S.p(2 * hk + j)], [Sbf.p(j)])
            for r in range(4):
                wt, jj = (wqk, r) if r < 2 else (wv, r - 2)
                gch = (hk, 16 + hk, 32 + 2 * hk, 33 + 2 * hk)[r]
                bank = self.mm_bank()
                for kc in range(16):
                    k.mm(bank[:, 0:512], wt[:, kc, jj * 128:(jj + 1) * 128], self.hT[:, kc, 0:512], kc == 0, kc == 15, [wt, self.hT], [bank])
                k.copy("act", raw[:, r, 3:515], bank[:, 0:512], [bank], [raw.p(r)])
                if first:
                    k.memset("pool", raw[:, r, 0:3], 0.0, [raw.p(r)])
                else:
                    k.copy("pool", raw[:, r, 0:3], self.qtail[:, gch, :], [self.qtail.p(gch)], [raw.p(r)])
                k.copy("pool", self.qtail[:, gch, :], raw[:, r, 512:515], [raw.p(r)], [self.qtail.p(gch)])
                cv = cvb[ci % 2]
                ci += 1
                k.ts("dve", cv[:], raw[:, r, 0:512], self.cw[:, gch, 0:1], None, ALU.mult, None, [raw.p(r), self.cw], [cv])
                for t in (1, 2, 3):
                    k.stt(cv[:], raw[:, r, t:t + 512], self.cw[:, gch, t:t + 1], cv[:], ALU.mult, ALU.add, [raw.p(r), self.cw, cv], [cv])
                k.act(cv[:], cv[:], AF.Silu, [cv], [cv])
                if r < 2:
                    k.act(sq[:], cv[:], AF.Square, [cv], [sq])
                    b2 = self.mm_bank()
                    k.mm(b2[:, 0:512], onesb[:], sq[:], True, True, [onesb, sq], [b2])
                    k.act(rinv[:], b2[:, 0:512], AF.Sqrt, [b2, self.small], [rinv], bias=self.epsc, scale=1.0)
                    k.recip(rinv[:], rinv[:], [rinv], [rinv])
                    if r == 0:
                        k.stt(QnT[:], cv[:], float(128 ** -0.5), rinv[:], ALU.mult, ALU.mult, [cv, rinv], [QnT])
                    else:
                        k.tt("dve", KnT[:], cv[:], rinv[:], ALU.mult, [cv, rinv], [KnT])
                else:
                    j = r - 2
                    h = 2 * hk + j
                    k.copy("act", vT[:, j, :], cv[:], [cv], [vT.p(j)])
                    for c in range(4):
                        tpp, tps = self.tp_slot()
                        k.tr(tps, vT[:, j, c * 128:(c + 1) * 128], self.identb[:], [vT.p(j), self.identb], [tpp])
                        k.act(Vs[:, c, j, :], tps, AF.Identity, [tpp, TP(7)], [Vs.p(c * 2 + j)], scale=tok[:, 7, c, h:h + 1])
            zb = [self.mm_bank(), self.mm_bank()]
            for s in range(4):
                bank = zb[s // 2]
                c0 = (s % 2) * 256
                for kc in range(16):
                    k.mm(bank[:, c0:c0 + 256], self.hT[:, kc, s * 128:(s + 1) * 128], wz[:, kc, :], kc == 0, kc == 15,
                         [wz, self.hT], [bank.p(2 * (s % 2), 2 * (s % 2) + 1)])
            for s in range(4):
                bank = zb[s // 2]
                c0 = (s % 2) * 256
                k.act(zg[:, s, :], bank[:, c0:c0 + 256], AF.Silu, [bank.p(2 * (s % 2), 2 * (s % 2) + 1)], [zg.p(s)])
                for j in range(2):
                    k.tt("pool", zg[:, s, j * 128:(j + 1) * 128], zg[:, s, j * 128:(j + 1) * 128], lay[:, 64:192], ALU.mult, [zg.p(s), lay], [zg.p(s)])
            for c in range(4):
                tpp, tps = self.tp_slot()
                k.tr(tps, KnT[:, c * 128:(c + 1) * 128], self.identb[:], [KnT, self.identb], [tpp])
                for j in range(2):
                    h = 2 * hk + j
                    k.act(Kd[:, c, j, :], tps, AF.Identity, [tpp, TP(8)], [Kd.p(c * 2 + j)], scale=tok[:, 8, c, h:h + 1])
            for j in range(2):
                h = 2 * hk + j
                cur = [0, 0, 0, 0]
                for c in range(4):
                    u = c
                    dg, (t1, t2, xx) = DG[u], TT_[u]
                    k.ts("dve", dg[:, 0:128], self.ident, tok[:, 2, c, h:h + 1], None, ALU.mult, None, [self.cst, TP(2)], [dg])
                    k.ts("dve", dg[:, 128:256], self.ident, tok[:, 0, c, h:h + 1], None, ALU.mult, None, [self.cst, TP(0)], [dg])
                    g1p, g1 = self.dr_slot()
                    k.mm(g1, self.ones, dg[:, 0:128], True, True, [self.cst, dg], [g1p])
                    g2p, g2 = self.dr_slot()
                    k.mm(g2, self.ones, dg[:, 128:256], True, True, [self.cst, dg], [g2p])
                    k.act(t1[:], g1, AF.Abs, [g1p, TP(3)], [t1], bias=tok[:, 3, c, h:h + 1], scale=1.0)
                    k.stt(t2[:], g2, tok[:, 0, c, h:h + 1], t1[:], ALU.add, ALU.subtract, [g2p, TP(0), t1], [t2])
                    k.act(t2[:], t2[:], AF.Exp, [t2], [t2])
                    k.act(t1[:], t1[:], AF.Exp, [t1, TP(0)], [t1], bias=tok[:, 0, c, h:h + 1], scale=-1.0)
                    kkp, kk = self.dr_slot()
                    k.mm(kk, KnT[:, c * 128:(c + 1) * 128], KnT[:, c * 128:(c + 1) * 128], True, True, [KnT], [kkp])
                    kqp, kq = self.dr_slot()
                    k.mm(kq, KnT[:, c * 128:(c + 1) * 128], QnT[:, c * 128:(c + 1) * 128], True, True, [KnT, QnT], [kqp])
                    k.tt("dve", xx[:], kk, t2[:], ALU.mult, [kkp, t2], [xx])
                    X0, Z0 = XZ[u][0], XZ[u][1]
                    k.tt("dve", X0[:], xx[:], self.negLs, ALU.mult, [xx, self.cst], [X0])
                    k.tt("pool", Z0[:], xx[:], self.negUs, ALU.mult, [xx, self.cst], [Z0])
                    k.tt("pool", PP[u][0][:], Z0[:], self.ident, ALU.add, [Z0, self.cst], [PP[u][0]])
                    k.tt("dve", t2[:], kq, t1[:], ALU.mult, [kqp, t1, t2], [t2])
                    k.tt("pool", QK[j][c][:], t2[:], self.Ui, ALU.mult, [t2, self.cst], [QK[j][c]])
                for lev in range(6):
                    for c in range(4):
                        u = c
                        a = cur[u]
                        Xk, Zk = XZ[u][2 * a], XZ[u][2 * a + 1]
                        Xn, Zn = XZ[u][2 * (1 - a)], XZ[u][2 * (1 - a) + 1]
                        pa = lev % 2
                        Pk, Pn = PP[u][pa], PP[u][1 - pa]
                        xp_, xps = self.dr_slot()
                        k.mm(xps, Zk[:], Xk[:], True, True, [Zk, Xk], [xp_])
                        if lev < 5:
                            zp_, zps = self.dr_slot()
                            k.mm(zps, Xk[:], Zk[:], True, True, [Zk, Xk], [zp_])
                        k.copy("act", Xn[:], xps, [xp_], [Xn])
                        if lev < 5:
                            k.copy("dve", Zn[:], zps, [zp_], [Zn])
                        pp_, pps = self.dr_slot()
                        k.mm(pps, Xn[:], Pk[:], True, True, [Xn, Pk], [pp_])
                        if lev < 5:
                            k.tt("dve", Pn[:], Pk[:], pps, ALU.add, [Pk, pp_], [Pn])
                        else:
                            k.tt("dve", PT[j][c][:], Pk[:], pps, ALU.add, [Pk, pp_], [PT[j][c]])
                        cur[u] = 1 - a
            for c in range(4):
                cs = slice(c * 128, (c + 1) * 128)
                for j in range(2):
                    h = 2 * hk + j
                    ksp, ks = self.dr_slot()
                    k.mm(ks, KnT[:, cs], Sbf[:, j, :], True, True, [KnT, Sbf.p(j)], [ksp])
                    k.stt(Rb[j][:], ks, tok[:, 6, c, h:h + 1], Vs[:, c, j, :], ALU.mult, ALU.add, [ksp, TP(6), Vs.p(c * 2 + j)], [Rb[j]])
                    yp_, ys_ = self.dr_slot()
                    k.mm(ys_, PT[j][c][:], Rb[j][:], True, True, [PT[j][c], Rb[j]], [yp_])
                    k.copy("act", Yb[j][:], ys_, [yp_], [Yb[j]])
                    qsp, qs = self.dr_slot()
                    k.mm(qs, QnT[:, cs], Sbf[:, j, :], True, True, [QnT, Sbf.p(j)], [qsp])
                    oip, oi = self.dr_slot()
                    k.mm(oi, QK[j][c][:], Yb[j][:], True, True, [QK[j][c], Yb[j]], [oip])
                    k.act(tO[j][:], qs, AF.Identity, [qsp, TP(5)], [tO[j]], scale=tok[:, 5, c, h:h + 1])
                    k.tt("dve", Of[j][:], oi, tO[j][:], ALU.add, [oip, tO[j]], [Of[j]])
                    ssc = self.small[:, 16 + j:17 + j]
                    k.memset("pool", ssc, 0.0, [self.small.p(2 + j)])
                    k.act(tO[j][:], Of[j][:], AF.Square, [Of[j]], [tO[j], self.small.p(2 + j)], accum_out=ssc)
                    k.act(ssc, ssc, AF.Sqrt, [self.small.p(0), self.small.p(2 + j)], [self.small.p(2 + j)], bias=self.epsc, scale=1.0 / 128)
                    k.recip(ssc, ssc, [self.small.p(2 + j)], [self.small.p(2 + j)])
                    k.stt(ogb[j][:], Of[j][:], ssc, zg[:, c, j * 128:(j + 1) * 128], ALU.mult, ALU.mult, [Of[j], self.small.p(2 + j), zg.p(c)], [ogb[j]])
                    if hk == 0 and c == 0 and j == 0:
                        self.dump("Of", Of[j], Of[j][:], [128, 128])
                        self.dump("ssc", self.small, self.small[:, 16:18], [128, 2])
                        self.dump("zg", zg, zg[:, 0, 0:128], [128, 128])
                        self.dump("ogb", ogb[j], ogb[j][:], [128, 128])
                        self.dump("Yb", Yb[j], Yb[j][:], [128, 128])
                        self.dump("lay", self.lay, self.lay[:, :], [128, 192])
                    tpp, tps = self.tp_slot()
                    k.tr(tps, ogb[j][:], self.identb[:], [ogb[j], self.identb], [tpp])
                    k.copy("act", oT[:, h, cs], tps, [tpp], [oT.p(h)])
                    sup, su = self.dr_slot()
                    k.mm(su, Kd[:, c, j, :], Yb[j][:], True, True, [Kd.p(c * 2 + j), Yb[j]], [sup])
                    k.stt(self.S[:, h, :], self.S[:, h, :], tok[:, 9, c, h:h + 1], su, ALU.mult, ALU.add, [self.S.p(h), TP(9), sup], [self.S.p(h)])
                    k.copy("act", Sbf[:, j, :], self.S[:, h, :], [self.S.p(h)], [Sbf.p(j)])
        k.barrier()
        if last:
            k.dma("sp", self.ndp[jl].rearrange("h k v -> k h v"), self.S[:, :, :], [self.S], [self.ndp], self.S)
            self.RX.reset()
            tl = [self.RX.alloc("tl%d" % i, [3, 2048], F32) for i in range(2)]
            for q in range(4):
                t = tl[q % 2]
                for cc in range(16):
                    c = q * 16 + cc
                    dp, ps = self.dr_slot()
                    k.tr(ps[0:3, :], self.qtail[:, c, :], self.ident, [self.qtail.p(c), self.cst], [dp])
                    k.copy("act", t[:, cc * 128:(cc + 1) * 128], ps[0:3, :], [dp], [t])
                k.dma("sp", self.nqp[jl, :, q * 2048:(q + 1) * 2048], t[:], [t], [self.nqp], t)
            k.barrier()
        self.dump("oT", oT, oT[:, :, :], [128, 32, 512])
        self.RO.reset()
        outbuf = self.RO.alloc("outbuf", [128, 4, D], F32, 4)
        self.out_proj(self.g_w_out, jl, 32, oT, nsub, R, outbuf)
        k.barrier()
        self.dump("outbuf", outbuf, outbuf[:, :, :], [128, 4, D])
        k.barrier()
        return outbuf

    def gdn_sample(self, l, jl):
        k = self.k
        NS = self.NS
        RO, RX, RA = self.RO, self.RX, self.RA
        lay = self.lay
        oTs = RX.alloc("oTs", [128, 32, NS], BF16)
        rawT = RX.alloc("rawT", [128, 64, NS], F32, 64)
        zT = RX.alloc("zT", [128, 32, NS], F32, 32)
        qkvT = RX.alloc("qkvT", [128, 64, NS], F32, 64)
        kq = RX.alloc("kq", [128, NS, 16, 2], F32)
        BCb = RX.alloc("BCb", [128, NS, 32], F32)
        BCa = RX.alloc("BCa", [128, NS, 32], F32)
        KQb = RX.alloc("KQb", [128, NS, 16], F32)
        OB = RX.alloc("OB", [128, 32, NS], F32)
        tk = RX.alloc("tk", [NS, 4, 32], F32)
        big = RX.alloc("bigd", [NS, 512], F32)
        v1 = RX.alloc("v1", [128, 32], F32)
        v2 = RX.alloc("v2", [128, 32], F32)
        v3 = RX.alloc("v3", [128, 32], F32)
        sqs = RX.alloc("sqs", [128, 512], F32)
        dgs = [RX.alloc("dgs%d" % i, [128, 128], F32) for i in range(2)]
        acc = RX.alloc("accs", [128, NS], F32)
        Sb = [RO.alloc("Sb0", [128, 32, 128], F32, 32), RA.alloc("Sb1", [128, 32, 128], F32, 32)]
        rown = [RO.alloc("rown%d" % i, [3 * NS, 2048], F32) for i in range(2)]
        bufT = RA.alloc("bufT", [128, 64, 3 * NS], F32, 64)
        stg = [rown[0]]
        for q in range(4):
            st = stg[0]
            k.dma("sp", st[:], self.sqkv[jl, :, q * 2048:(q + 1) * 2048], [], [st], st)
            for cc in range(16):
                c = q * 16 + cc
                dp, ps = self.dr_slot()
                k.tr(ps[:, 0:3 * NS], st[:, cc * 128:(cc + 1) * 128], self.ident[0:3 * NS, 0:3 * NS], [st, self.cst], [dp])
                k.copy("dve", bufT[:, c, :], ps[:, 0:3 * NS], [dp], [bufT.p(c)])
        k.barrier()
        for hk in range(NKH):
            wqk = self.piece([self.wspec(self.w_qkvz, jl, 0, 16, hk * 128, 128),
                              self.wspec(self.w_qkvz, jl, 0, 16, 2048 + hk * 128, 128)])
            wv = self.piece([self.wspec(self.w_qkvz, jl, 0, 16, 4096 + hk * 256, 256)])
            wz = self.piece([self.wspec(self.w_qkvz, jl, 0, 16, 8192 + hk * 256, 256)])

            def craw(chs):
                def f(j, bank):
                    k.copy("act", rawT[:, chs[j], :], bank[:, 0:NS], [bank], [rawT.p(chs[j])])
                return f

            def cz(chs):
                def f(j, bank):
                    k.act(zT[:, chs[j], :], bank[:, 0:NS], AF.Silu, [bank], [zT.p(chs[j])])
                return f
            self.fm_chunks(wqk, 2, 16, NS, craw([hk, 16 + hk]))
            self.fm_chunks(wv, 2, 16, NS, craw([32 + 2 * hk, 33 + 2 * hk]))
            self.fm_chunks(wz, 2, 16, NS, cz([2 * hk, 2 * hk + 1]))
        w = self.piece([self.wspec(self.w_ba, jl, 0, 16, 0, 64)])
        bank = self.mm_bank()
        for kc in range(16):
            k.mm(bank[0:NS, 0:64], self.hT[:, kc, 0:NS], w[:, kc, 0:64], kc == 0, kc == 15, [w, self.hT], [bank])
        k.act(tk[:, 0, :], bank[0:NS, 0:32], AF.Exp, [bank], [tk], scale=-1.0)
        k.act(tk[:, 0, :], tk[:, 0, :], AF.Ln, [tk], [tk], bias=1.0)
        k.act(tk[:, 0, :], tk[:, 0, :], AF.Exp, [tk], [tk], scale=-1.0)
        k.tt("dve", tk[:, 1, :], bank[0:NS, 32:64], lay[0:NS, 0:NH], ALU.add, [bank, lay], [tk])
        k.act(tk[:, 1, :], tk[:, 1, :], AF.Exp, [tk], [tk])
        k.act(tk[:, 1, :], tk[:, 1, :], AF.Ln, [tk], [tk], bias=1.0)
        k.tt("dve", tk[:, 1, :], tk[:, 1, :], lay[0:NS, NH:2 * NH], ALU.mult, [tk, lay], [tk])
        k.act(tk[:, 1, :], tk[:, 1, :], AF.Exp, [tk], [tk])
        for qi, dst in ((0, BCb), (1, BCa)):
            bv = big[:, :].rearrange("p (b h) -> p b h", h=32)
            for b in range(NS):
                k.ts("dve", bv[:, b, :], tk[:, qi, :], self.ident[0:NS, b:b + 1], None, ALU.mult, None, [tk, self.cst, big], [big])
            bank = self.mm_bank()
            k.mm(bank[:, 0:512], self.ones[0:NS, :], big[:, :], True, True, [self.cst, big], [bank])
            k.copy("act", dst[:, :, :], bank[:, 0:512].rearrange("p (b h) -> p b h", h=32), [bank], [dst])
        for c in range(64):
            bv = bufT[:, c, :].rearrange("p (b j) -> p b j", j=3)
            k.ts("dve", acc[:], bv[:, :, 0], self.cw[:, c, 0:1], None, ALU.mult, None, [bufT.p(c), self.cw], [acc])
            k.stt(acc[:], bv[:, :, 1], self.cw[:, c, 1:2], acc[:], ALU.mult, ALU.add, [bufT.p(c), self.cw, acc], [acc])
            k.stt(acc[:], bv[:, :, 2], self.cw[:, c, 2:3], acc[:], ALU.mult, ALU.add, [bufT.p(c), self.cw, acc], [acc])
            k.stt(acc[:], rawT[:, c, :], self.cw[:, c, 3:4], acc[:], ALU.mult, ALU.add, [rawT.p(c), self.cw, acc], [acc])
            k.act(qkvT[:, c, :], acc[:], AF.Silu, [acc], [qkvT.p(c)])
        qk2 = qkvT[:, 0:32, :]
        k.tt("dve", sqs[:, :].rearrange("p (c b) -> p c b", b=NS), qk2, qk2, ALU.mult, [qkvT], [sqs])
        bank = self.mm_bank()
        k.mm(bank[:, 0:512], self.ones, sqs[:], True, True, [self.cst, sqs], [bank])
        k.act(sqs[:], bank[:, 0:512], AF.Sqrt, [bank, self.small], [sqs], bias=self.epsc, scale=1.0)
        k.recip(sqs[:], sqs[:], [sqs], [sqs])
        rv = sqs[:, :].rearrange("p (c b) -> p c b", b=NS)
        kqv_k = kq[:, :, :, 0].rearrange("p b h -> p h b")
        kqv_q = kq[:, :, :, 1].rearrange("p b h -> p h b")
        k.stt(kqv_q, qkvT[:, 0:16, :], float(128 ** -0.5), rv[:, 0:16, :], ALU.mult, ALU.mult, [qkvT, sqs], [kq])
        k.tt("dve", kqv_k, qkvT[:, 16:32, :], rv[:, 16:32, :], ALU.mult, [qkvT, sqs], [kq])
        sq2 = sqs[:, 0:NS * 16].rearrange("p (b h) -> p b h", h=16)
        k.tt("dve", sq2, kq[:, :, :, 0], kq[:, :, :, 1], ALU.mult, [kq, sqs], [sqs])
        bank = self.mm_bank()
        k.mm(bank[:, 0:NS * 16], self.ones, sqs[:, 0:NS * 16], True, True, [self.cst, sqs], [bank])
        k.copy("act", KQb[:, :, :], bank[:, 0:NS * 16].rearrange("p (b h) -> p b h", h=16), [bank], [KQb])
        k.barrier()
        dstv = self.nqs[jl].rearrange("(b j) d -> b j d", j=3)
        srcv = self.sqkv[jl].rearrange("(b j) d -> b j d", j=3)
        for q in range(4):
            rn = rown[q % 2]
            for cc in range(16):
                c = q * 16 + cc
                dp, ps = self.dr_slot()
                k.tr(ps[0:NS, :], rawT[:, c, :], self.ident, [rawT.p(c), self.cst], [dp])
                k.copy("act", rn[0:NS, cc * 128:(cc + 1) * 128], ps[0:NS, :], [dp], [rn])
            k.dma("sp", dstv[:, 2, q * 2048:(q + 1) * 2048], rn[0:NS, :], [rn], [self.nqs], rn)
        k.dma("sp", dstv[:, 0:2, :], srcv[:, 1:3, :], [], [self.nqs], rown[0])
        k.barrier()
        for b in range(NS):
            sb = Sb[b % 2]
            k.dma("sp", sb[:, :, :], self.sdel[jl, b].rearrange("h k v -> k h v"), [], [sb], sb)
            dp, ps = self.dr_slot()
            psv = ps[:, 0:64].rearrange("p (h t) -> p h t", t=2)
            for h in range(NH):
                k.mm(ps[:, 2 * h:2 * h + 2], sb[:, h, :], kq[:, b, h // 2, :], True, True, [sb.p(h), kq], [dp])
            k.mm(ps[:, 64:128], self.ident, self.ident[:, 0:64], True, True, [self.cst], [dp])
            k.tt("dve", v1[:], psv[:, :, 0], BCa[:, b, :], ALU.mult, [dp, BCa], [v1])
            k.tt("dve", v1[:], qkvT[:, 32:64, b], v1[:], ALU.subtract, [qkvT, v1], [v1])
            k.tt("dve", v1[:], v1[:], BCb[:, b, :], ALU.mult, [v1, BCb], [v1])
            k.tt("dve", v2[:], psv[:, :, 1], BCa[:, b, :], ALU.mult, [dp, BCa], [v2])
            v1v = v1[:, :].rearrange("p (h t) -> p h t", t=2)
            v3v = v3[:, :].rearrange("p (h t) -> p h t", t=2)
            for t in range(2):
                k.tt("dve", v3v[:, :, t], v1v[:, :, t], KQb[:, b, :], ALU.mult, [v1, KQb], [v3])
            k.tt("dve", OB[:, :, b], v3[:], v2[:], ALU.add, [v3, v2], [OB])
            for h in range(NH):
                dg = dgs[h % 2]
                k.ts("dve", dg[:], self.ident, v1[:, h:h + 1], None, ALU.mult, None, [self.cst, v1], [dg])
                vp, vps = self.dr_slot()
                k.mm(vps, self.ones, dg[:], True, True, [self.cst, dg], [vp])
                k.act(sb[:, h, :], sb[:, h, :], AF.Identity, [sb.p(h), BCa], [sb.p(h)], scale=BCa[:, b, h:h + 1])
                k.stt(sb[:, h, :], vps, kq[:, b, h // 2, 0:1], sb[:, h, :], ALU.mult, ALU.add, [vp, kq, sb.p(h)], [sb.p(h)])
            k.dma("sp", self.nds[jl, b].rearrange("h k v -> k h v"), sb[:, :, :], [sb], [self.nds], sb)
        k.tt("dve", sqs[:, :].rearrange("p (h b) -> p h b", b=NS), OB[:, :, :], OB[:, :, :], ALU.mult, [OB], [sqs])
        bank = self.mm_bank()
        k.mm(bank[:, 0:512], self.ones, sqs[:], True, True, [self.cst, sqs], [bank])
        k.act(sqs[:], bank[:, 0:512], AF.Sqrt, [bank, self.small], [sqs], bias=self.epsc, scale=1.0 / 128)
        k.recip(sqs[:], sqs[:], [sqs], [sqs])
        nwc = self.small[:, 4:5]
        k.tt("dve", dgs[0][:], lay[:, 64:192], self.ident, ALU.mult, [lay, self.cst], [dgs[0]])
        k.op("dve", lambda e, o=nwc, i=dgs[0][:]: e.tensor_reduce(out=o, in_=i, axis=AX.X, op=ALU.add), [dgs[0]], [self.small.p(4)])
        k.stt(sqs[:, :].rearrange("p (h b) -> p h b", b=NS), OB[:, :, :], nwc, sqs[:, :].rearrange("p (h b) -> p h b", b=NS),
              ALU.mult, ALU.mult, [OB, self.small.p(4), sqs], [sqs])
        k.tt("dve", oTs[:, :, :], sqs[:, :].rearrange("p (h b) -> p h b", b=NS), zT[:, :, :], ALU.mult, [sqs, zT], [oTs])
        k.barrier()
        RO.reset()
        outbuf = RO.alloc("outbuf", [128, 4, D], F32, 4)
        self.out_proj(self.g_w_out, jl, 32, oTs, 1, NS, outbuf)
        k.barrier()
        return outbuf


def make_consts():
    p = np.arange(128)[:, None]
    f = np.arange(128)[None, :]
    mats = [(p == f), -1.0 * (p > f), -1.0 * (p < f), (p <= f), (p == 127) * np.ones((128, 128)), np.ones((128, 128))]
    return np.ascontiguousarray(np.concatenate([np.asarray(m, dtype=np.float32) for m in mats], axis=1))


_GEN_CACHE = {}
_LAST_DBG = {}


def run_step(inp, NP, NS, DEPTH, ncores, nseq, stop_after=None):
    key = (NP, NS, DEPTH)
    if key not in _GEN_CACHE:
        _GEN_CACHE[key] = Gen(NP, NS, DEPTH, stop_after)
    gen = _GEN_CACHE[key]
    NA, NB = (DEPTH + 1) // 2, DEPTH // 2
    f = lambda a: np.ascontiguousarray(np.asarray(a, dtype=np.float32))
    cst = make_consts()
    shared = {}
    for n in ("w_ada", "b_ada", "norm_gain", "w_up", "w_down", "gdn_w_qkvz", "gdn_w_ba", "gdn_conv_w", "gdn_w_out", "sc_w_in", "sc_conv_w", "sc_w_out"):
        shared[n] = f(inp[n])
    if NB == 0:
        shared["sc_w_in"] = np.zeros((1, D, 3 * D), np.float32)
        shared["sc_conv_w"] = np.zeros((1, 3, D), np.float32)
        shared["sc_w_out"] = np.zeros((1, D, D), np.float32)
    gsm = np.zeros((NA, 1024), np.float32)
    gsm[:, 0:32] = f(inp["gdn_dt_bias"])
    gsm[:, 32:64] = f(inp["gdn_a_log"])
    gsm[:, 64:192] = f(inp["gdn_norm"])
    shared["gsm"] = gsm
    shared["cst"] = cst
    in_maps = []
    for c in range(ncores):
        sq = c % nseq
        sl = slice(c * NS, (c + 1) * NS)
        m = dict(shared)
        m["xp"] = f(inp["x_prompt"][sq])
        m["xs"] = f(inp["x_sample"][sl, 0, :])
        m["cv"] = f(np.concatenate([inp["c_sample"][sl], inp["c_prompt"][sq:sq + 1]], axis=0))
        m["sdel"] = f(inp["state_delta"][:, sl])
        m["sqkv"] = f(np.asarray(inp["state_qkv_conv"])[:, sl].reshape(NA, NS * 3, CONVD))
        if NB:
            m["ssc"] = f(np.asarray(inp["state_short_conv"])[:, sl].reshape(NB, NS * 2, D))
        else:
            m["ssc"] = np.zeros((1, NS * 2, D), np.float32)
        in_maps.append(m)
    res = run_bass_kernel_spmd(gen.nc, in_maps, core_ids=list(range(ncores)))
    r = res.results
    _LAST_DBG.clear()
    _LAST_DBG.update({n: r[0]["dbg_" + n] for n in gen.dbg})
    y_p = np.stack([r[c]["yp"] for c in range(nseq)], axis=0)
    y_s = np.concatenate([r[c]["ys"] for c in range(ncores)], axis=0)[:, None, :]
    nd_p = np.stack([r[c]["ndp"] for c in range(nseq)], axis=1)
    nq_p = np.stack([r[c]["nqp"] for c in range(nseq)], axis=1)
    ns_p = np.stack([r[c]["nsp"][:NB] for c in range(nseq)], axis=1)
    nd_s = np.concatenate([r[c]["nds"] for c in range(ncores)], axis=1)
    nq_s = np.concatenate([r[c]["nqs"].reshape(NA, NS, 3, CONVD) for c in range(ncores)], axis=1)
    ns_s = np.concatenate([r[c]["nss"][:NB].reshape(NB, NS, 2, D) for c in range(ncores)], axis=1)
    return tuple(np.ascontiguousarray(a, dtype=np.float32) for a in (y_p, y_s, nd_p, nq_p, ns_p, nd_s, nq_s, ns_s))


def kernel(**inputs):
    return run_step(inputs, 2048, 16, 4, 8, 4)
```

```python
import contextlib
import numpy as np
import concourse.bass as bass
import concourse.mybir as mybir

F32 = mybir.dt.float32
BF16 = mybir.dt.bfloat16
ALU = mybir.AluOpType
AF = mybir.ActivationFunctionType
AX = mybir.AxisListType


class Dep:
    __slots__ = ("w", "r")

    def __init__(self):
        self.w = None
        self.r = {}


class Prod:
    def __init__(self, sem, inc, name):
        self.sem = sem
        self.inc = inc
        self.count = 0
        self.name = name
        self.is_dma = inc == 16


class Tile:
    def __init__(self, k, tensor, nparts=1, name=""):
        self.k = k
        self.t = tensor
        self.name = name
        self.deps = [Dep() for _ in range(nparts)]
        self.chan = None
        self.excl = False

    def __getitem__(self, idx):
        return self.t[idx]

    def p(self, *idx):
        if self.excl:
            return self
        return Part(self, idx)

    def dep_list(self):
        return self.deps

    def get_chan(self):
        if self.chan is None:
            self.chan = self.k.new_chan(self.name)
        return self.chan


class Part:
    def __init__(self, tile, idx):
        self.tile = tile
        self.idx = idx

    def dep_list(self):
        return [self.tile.deps[i] for i in self.idx]

    def get_chan(self):
        return self.tile.get_chan()


class K:
    ENGS = ("pe", "act", "dve", "pool", "sp")

    def __init__(self, nc, same_engine_sync=True):
        self.nc = nc
        self.stack = contextlib.ExitStack()
        self.q = {e: [] for e in self.ENGS}
        self.prod = {}
        self.seen = {e: {} for e in self.ENGS}
        self.same_engine_sync = same_engine_sync
        self.chans = []
        self.chan_by_name = {}
        self.nins = 0
        for e in ("pe", "act", "dve", "pool"):
            sem = self.stack.enter_context(nc.semaphore("s_" + e))
            self.prod[e] = Prod(sem, 1, e)

    def new_chan(self, name):
        if name in self.chan_by_name:
            return self.chan_by_name[name]
        sem = self.stack.enter_context(self.nc.semaphore("d%d_%s" % (len(self.chans), name)))
        c = Prod(sem, 16, "dma_" + name)
        self.chans.append(c)
        self.chan_by_name[name] = c
        return c

    def sbuf(self, name, shape, dtype, nparts=1):
        t = self.stack.enter_context(self.nc.sbuf_tensor(name, list(shape), dtype))
        return Tile(self, t, nparts, name)

    def psum(self, name, shape, dtype, nparts=1):
        t = self.stack.enter_context(self.nc.psum_tensor(name, list(shape), dtype))
        tl = Tile(self, t, 1, name)
        tl.excl = True
        return tl

    def dram(self, name, shape, dtype, kind, nparts=1):
        t = self.nc.dram_tensor(name, list(shape), dtype, kind=kind)
        return Tile(self, t, nparts, name)

    def _collect(self, reads, writes):
        need = {}
        raw = {}

        def add(pc, dst):
            if pc is None:
                return
            p, c = pc
            if dst.get(p, 0) < c:
                dst[p] = c

        for t in reads:
            for d in t.dep_list():
                add(d.w, need)
                add(d.w, raw)
                if getattr(t, "excl", False):
                    for p, c in d.r.items():
                        add((p, c), need)
        for t in writes:
            for d in t.dep_list():
                add(d.w, need)
                for p, c in d.r.items():
                    add((p, c), need)
        self._raw = raw
        return need

    def _emit_waits(self, en, need, own):
        seen = self.seen[en]
        for p, c in need.items():
            if p is own and not p.is_dma:
                if en == "pe" or not self.same_engine_sync:
                    continue
                c = self._raw.get(p, 0)
                if c == 0:
                    continue
            if p.is_dma:
                c = max(c, p.count)
            if seen.get(p, 0) >= c:
                continue
            seen[p] = c
            self.q[en].append(("wait", p.sem, c))

    def _mark(self, reads, writes, p, c):
        for t in reads:
            for d in t.dep_list():
                if d.r.get(p, 0) < c:
                    d.r[p] = c
        for t in writes:
            for d in t.dep_list():
                d.w = (p, c)
                d.r = {}

    def op(self, en, fn, reads=(), writes=()):
        p = self.prod[en]
        need = self._collect(reads, writes)
        self._emit_waits(en, need, p)
        p.count += 1
        self.q[en].append(("ins", fn, p.sem, 1))
        self._mark(reads, writes, p, p.count)
        self.nins += 1

    def dma(self, qn, out_ap, in_ap, reads, writes, chan_owner, **kw):
        ch = chan_owner.get_chan()
        need = self._collect(reads, writes)
        self._emit_waits(qn, need, None)
        ch.count += 16
        self.q[qn].append(("ins", lambda e: e.dma_start(out=out_ap, in_=in_ap, **kw), ch.sem, 16))
        self._mark(reads, writes, ch, ch.count)
        self.nins += 1


    def barrier(self):
        allp = [self.prod[e] for e in ("pe", "act", "dve", "pool")] + self.chans
        for en in self.ENGS:
            seen = self.seen[en]
            for p in allp:
                if p is self.prod.get(en):
                    continue
                if p.count and seen.get(p, 0) < p.count:
                    seen[p] = p.count
                    self.q[en].append(("wait", p.sem, p.count))

    def mm(self, out, lhsT, rhs, start, stop, reads, writes):
        self.op("pe", lambda e, o=out, l=lhsT, r=rhs, a=start, b=stop: e.matmul(o, lhsT=l, rhs=r, start=a, stop=b), reads, writes)

    def tr(self, out, in_, ident, reads, writes):
        self.op("pe", lambda e, o=out, i=in_, d=ident: e.transpose(o, i, d), reads, writes)

    def act(self, out, in_, func, reads, writes, **kw):
        self.op("act", lambda e, o=out, i=in_, f=func, kw=kw: e.activation(out=o, in_=i, func=f, **kw), reads, writes)

    def copy(self, en, out, in_, reads, writes):
        if en == "act":
            self.op("act", lambda e, o=out, i=in_: e.copy(out=o, in_=i), reads, writes)
        else:
            self.op(en, lambda e, o=out, i=in_: e.tensor_copy(out=o, in_=i), reads, writes)

    def tt(self, en, out, in0, in1, op, reads, writes):
        self.op(en, lambda e, o=out, a=in0, b=in1, p=op: e.tensor_tensor(out=o, in0=a, in1=b, op=p), reads, writes)

    def ts(self, en, out, in0, s1, s2, op0, op1, reads, writes):
        if s2 is None:
            s2, op1 = 0.0, ALU.add
        self.op(en, lambda e, o=out, a=in0, x=s1, y=s2, p0=op0, p1=op1: e.tensor_scalar(out=o, in0=a, scalar1=x, scalar2=y, op0=p0, op1=p1), reads, writes)

    def stt(self, out, in0, scalar, in1, op0, op1, reads, writes):
        self.op("dve", lambda e, o=out, a=in0, s=scalar, b=in1, p0=op0, p1=op1: e.scalar_tensor_tensor(out=o, in0=a, scalar=s, in1=b, op0=p0, op1=p1), reads, writes)

    def recip(self, out, in_, reads, writes):
        self.op("dve", lambda e, o=out, i=in_: e.reciprocal(out=o, in_=i), reads, writes)

    def memset(self, en, ap, val, writes):
        self.op(en, lambda e, a=ap, v=val: e.memset(a, v), [], writes)


    def finish(self):
        for ch in self.chans:
            if ch.count:
                self.q["sp"].append(("wait", ch.sem, ch.count))
        for e in ("pe", "act", "dve", "pool"):
            p = self.prod[e]
            if p.count:
                self.q["sp"].append(("wait", p.sem, p.count))
        nc = self.nc
        q = self.q

        def run(lst, eng):
            for it in lst:
                if it[0] == "wait":
                    eng.wait_ge(it[1], it[2])
                else:
                    ins = it[1](eng)
                    ins.then_inc(it[2], it[3])

        with nc.Block() as block:
            @block.sync
            def _(e):
                run(q["sp"], e)

            @block.scalar
            def _(e):
                run(q["act"], e)

            @block.vector
            def _(e):
                run(q["dve"], e)

            @block.gpsimd
            def _(e):
                run(q["pool"], e)

            @block.tensor
            def _(e):
                run(q["pe"], e)
        self.stack.close()


class Region:
    def __init__(self, k, name, nwords):
        self.k = k
        self.t = k.stack.enter_context(k.nc.sbuf_tensor(name, [128, nwords], F32))
        self.n = nwords
        self.off = 0
        self.name = name

    def reset(self):
        self.off = 0

    def alloc(self, name, shape, dtype, nparts=1):
        free = 1
        for d in shape[1:]:
            free *= d
        words = free if dtype == F32 else (free + 1) // 2
        assert self.off + words <= self.n, (self.name, name, self.off, words, self.n)
        ap = self.t[0:shape[0], self.off:self.off + words]
        self.off += words
        if dtype == BF16:
            ap = ap.bitcast(BF16)
        if len(shape) == 3:
            ap = ap.rearrange("p (a b) -> p a b", b=shape[2])
        elif len(shape) == 4:
            ap = ap.rearrange("p (a b c) -> p a b c", b=shape[2], c=shape[3])
        return Tile(self.k, ap, nparts, name)
from concourse.bass_utils import run_bass_kernel_spmd
D = 2048
NH = 32
NKH = 16
DFF = 8192
CONVD = 8192
EPS = 1e-6
NW = 3


class Gen:
    def __init__(self, NP, NS, DEPTH, stop_after=None):
        self.NP, self.NS, self.DEPTH = NP, NS, DEPTH
        self.NA, self.NB = (DEPTH + 1) // 2, DEPTH // 2
        NA, NB = self.NA, self.NB
        nc = bass.Bass("TRN2", target_bir_lowering=False)
        self.nc = nc
        k = K(nc)
        self.k = k

        def di(n, s, parts=1):
            return k.dram(n, s, F32, "ExternalInput", parts)

        def do(n, s, parts=1):
            return k.dram(n, s, F32, "ExternalOutput", parts)

        self.xp = di("xp", [NP, D])
        self.xs = di("xs", [NS, D])
        self.cv = di("cv", [NS + 1, D])
        self.sdel = di("sdel", [NA, NS, NH, 128, 128])
        self.sqkv = di("sqkv", [NA, NS * 3, CONVD])
        self.ssc = di("ssc", [max(NB, 1), NS * 2, D])
        self.w_ada = di("w_ada", [DEPTH, D, 6 * D])
        self.b_ada = di("b_ada", [DEPTH, 6 * D])
        self.norm_gain = di("norm_gain", [DEPTH, 4, D])
        self.w_up = di("w_up", [DEPTH, D, DFF])
        self.w_down = di("w_down", [DEPTH, DFF, D])
        self.w_qkvz = di("gdn_w_qkvz", [NA, D, 12288])
        self.w_ba = di("gdn_w_ba", [NA, D, 64])
        self.g_conv_w = di("gdn_conv_w", [NA, 4, CONVD])
        self.gsm = di("gsm", [NA, 1024])
        self.g_w_out = di("gdn_w_out", [NA, 4096, D])
        self.sc_w_in = di("sc_w_in", [max(NB, 1), D, 3 * D])
        self.sc_conv_w = di("sc_conv_w", [max(NB, 1), 3, D])
        self.sc_w_out = di("sc_w_out", [max(NB, 1), D, D])
        self.cst_d = di("cst", [128, 6 * 128])

        self.yp = do("yp", [NP, D])
        self.ys = do("ys", [NS, D])
        self.ndp = do("ndp", [NA, NH, 128, 128])
        self.nqp = do("nqp", [NA, 3, CONVD])
        self.nsp = do("nsp", [max(NB, 1), 2, D])
        self.nds = do("nds", [NA, NS, NH, 128, 128])
        self.nqs = do("nqs", [NA, NS * 3, CONVD])
        self.nss = do("nss", [max(NB, 1), NS * 2, D])
        self.xscr = k.dram("xscr", [NP + NS, D], F32, "Internal", NP // 128 + 1)
        self.modscr = k.dram("modscr", [DEPTH, NS + 1, 6 * D], F32, "Internal", DEPTH)
        self.NPIECE = NA * 65 + DEPTH * 64 + NB * 32 + 4
        self.wscr_l = [k.dram("wscr%d" % i, [200, 128, 4096], BF16, "Internal", 200) for i in range((self.NPIECE + 199) // 200)]
        self.wkeys = {}

        class _Own:
            def __init__(self, kk, name):
                self.kk, self.name, self.chan = kk, name, None

            def get_chan(self):
                if self.chan is None:
                    self.chan = self.kk.new_chan(self.name)
                return self.chan
        self.wconv = [_Own(k, "wconv%d" % i) for i in range(8)]

        self.cst = k.sbuf("cst_sb", [128, 6 * 128], F32)
        self.identb = k.sbuf("identb", [128, 128], BF16)
        self.S = k.sbuf("S_state", [128, NH, 128], F32, NH)
        self.qtail = k.sbuf("qtail", [128, 64, 3], F32, 64)
        self.sctail = k.sbuf("sctail", [128, 16, 2], F32, 16)
        self.cw = k.sbuf("cw", [128, 64, 4], F32)
        self.scw = k.sbuf("scw", [128, 16, 3], F32)
        self.small = k.sbuf("small", [128, 64], F32, 8)
        self.lay = k.sbuf("lay", [128, 2 * NH + 128], F32)
        self.tok = k.sbuf("tok", [128, 12, 4, NH], F32, 12)
        self.hT = k.sbuf("hT", [128, 16, 512], BF16, 4)
        self.wp = [k.sbuf("wp%d" % i, [128, 16, 256], BF16) for i in range(NW)]
        self.wi = 0
        self.RX = Region(k, "RX", 9216)
        self.RO = Region(k, "RO", 8192)
        self.RA = Region(k, "RA", 8192)
        self.MM = [k.psum("mm%d" % i, [128, 512], F32, 4) for i in range(3)]
        self.mmi = 0
        self.TPB = [k.psum("tpb%d" % i, [128, 1024], BF16, 8) for i in range(2)]
        self.tpi = 0
        self.DR = [k.psum("dr%d" % i, [128, 512], F32, 4) for i in range(3)]
        self.dri = 0

        c = self.cst
        self.ident = c[:, 0:128]
        self.negLs = c[:, 128:256]
        self.negUs = c[:, 256:384]
        self.Ui = c[:, 384:512]
        self.SelLast = c[:, 512:640]
        self.ones = c[:, 640:768]
        k.dma("sp", c[:], self.cst_d[:], [], [c], c)
        k.copy("dve", self.identb[:], self.ident, [c], [self.identb])
        k.memset("dve", self.small[:, 0:1], EPS, [self.small.p(0)])
        self.epsc = self.small[:, 0:1]

        self.stop_after = stop_after
        self.stage = 0
        self.dbg = {}
        try:
            self.build(stop_after)
        except StopIteration:
            pass
        k.finish()

    def mm_bank(self):
        b = self.MM[self.mmi % 3]
        self.mmi += 1
        return b

    def dr_slot(self):
        i = self.dri % 12
        self.dri += 1
        t = self.DR[i % 3]
        q = i // 3
        return t.p(q), t[:, q * 128:(q + 1) * 128]

    def tp_slot(self):
        i = self.tpi % 16
        self.tpi += 1
        t = self.TPB[i % 2]
        q = i // 2
        return t.p(q), t[:, q * 128:(q + 1) * 128]

    def conv_piece(self, specs):
        k = self.k
        key = tuple((wt.name, l, r0, nk, c0, nc_) for (wt, l, r0, nk, c0, nc_) in specs)
        nk = specs[0][3]
        tot = sum(sp_[5] for sp_ in specs)
        if key not in self.wkeys:
            idx = len(self.wkeys)
            assert idx < self.NPIECE
            self.wkeys[key] = idx
            wsc = self.wscr_l[idx // 200]
            dst = wsc[idx % 200][:, 0:nk * tot].rearrange("p (kc n) -> p kc n", n=tot)
            col = 0
            own = self.wconv[idx % 8]
            ch = own.get_chan()
            if ch.count and k.seen["pool"].get(ch, 0) < ch.count:
                k.seen["pool"][ch] = ch.count
                k.q["pool"].append(("wait", ch.sem, ch.count))
            for (wt, l, r0, nk_, c0, nc_) in specs:
                k.dma("pool", dst[:, :, col:col + nc_], self.wview(wt, l, r0, nk_, c0, nc_), [], [wsc.p(idx % 200)], own)
                col += nc_
        return self.wkeys[key], nk, tot

    def piece(self, specs):
        k = self.k
        idx, nk, tot = self.conv_piece(specs)
        w = self.wp[self.wi % NW]
        self.wi += 1
        wsc = self.wscr_l[idx // 200]
        src = wsc[idx % 200][:, 0:nk * tot].rearrange("p (kc n) -> p kc n", n=tot)
        k.dma("sp", w[:, 0:nk, 0:tot], src, [wsc.p(idx % 200)], [w], w)
        return w

    def preconvert(self, l):
        jl = l // 2
        W = self.wspec
        if l % 2 == 0:
            self.conv_piece([W(self.w_ba, jl, 0, 16, 0, 64)])
            for hk in range(NKH):
                self.conv_piece([W(self.w_qkvz, jl, 0, 16, hk * 128, 128), W(self.w_qkvz, jl, 0, 16, 2048 + hk * 128, 128)])
                self.conv_piece([W(self.w_qkvz, jl, 0, 16, 4096 + hk * 256, 256)])
                self.conv_piece([W(self.w_qkvz, jl, 0, 16, 8192 + hk * 256, 256)])
            for cb in range(8):
                for pi in range(2):
                    self.conv_piece([W(self.g_w_out, jl, pi * 2048, 16, cb * 256, 256)])
        else:
            for jp in range(8):
                for base in (0, 2048, 4096):
                    self.conv_piece([W(self.sc_w_in, jl, 0, 16, base + jp * 256, 256)])
            for cb in range(8):
                self.conv_piece([W(self.sc_w_out, jl, 0, 16, cb * 256, 256)])
        for hf in range(2):
            for pc in range(16):
                self.conv_piece([W(self.w_up, l, 0, 16, hf * 4096 + pc * 256, 256)])
            for cb in range(8):
                for pi in range(2):
                    self.conv_piece([W(self.w_down, l, hf * 4096 + pi * 2048, 16, cb * 256, 256)])

    def wspec(self, wt, l, r0, nk, c0, ncols):
        return (wt, l, r0, nk, c0, ncols)

    def wview(self, wt, l, r0, nk, c0, ncols):
        return wt[l, r0:r0 + nk * 128, c0:c0 + ncols].rearrange("(kc p) n -> p kc n", p=128)

    def rstd_col(self, ss, scale):
        k = self.k
        R = ss.shape[0]
        k.act(ss, ss, AF.Sqrt, [self.small], [self.small.p(1)], bias=self.epsc[0:R, :], scale=scale)
        k.recip(ss, ss, [self.small.p(1)], [self.small.p(1)])

    def adaln(self):
        k = self.k
        NS = self.NS
        R = NS + 1
        self.RO.reset()
        self.RX.reset()
        c17 = self.RX.alloc("c17", [R, D], F32)
        scT = self.RX.alloc("scT", [128, 16, R], F32)
        wA = [self.RO.alloc("wA%d" % i, [128, 16, 256], F32) for i in range(2)]
        bias = [self.RX.alloc("bias%d" % i, [1, 256], F32) for i in range(2)]
        msb = [self.RX.alloc("msb%d" % i, [R, 256], F32) for i in range(2)]
        k.dma("sp", c17[:], self.cv[:], [], [c17], c17)
        k.act(c17[:], c17[:], AF.Silu, [c17], [c17])
        for kc in range(16):
            dp, ps = self.dr_slot()
            k.tr(ps[:, 0:R], c17[0:R, kc * 128:(kc + 1) * 128], self.ident[0:R, 0:R], [c17, self.cst], [dp])
            k.copy("dve", scT[:, kc, :], ps[:, 0:R], [dp], [scT])
        it = 0
        for l in range(self.DEPTH):
            for blk in range(48):
                w = wA[it % 2]
                b = bias[it % 2]
                m = msb[it % 2]
                it += 1
                k.dma("sp", w[:], self.wview(self.w_ada, l, 0, 16, blk * 256, 256), [], [w], w)
                k.dma("sp", b[:], self.b_ada[l:l + 1, blk * 256:(blk + 1) * 256], [], [b], b)
                bank = self.mm_bank()
                for kc in range(16):
                    k.mm(bank[0:R, 0:256], scT[:, kc, :], w[:, kc, :], kc == 0, False, [scT, w], [bank])
                k.mm(bank[0:R, 0:256], self.ones[0:1, 0:R], b[0:1, :], False, True, [self.cst, b], [bank])
                k.copy("act", m[:], bank[0:R, 0:256], [bank], [m])
                k.dma("sp", self.modscr[l, :, blk * 256:(blk + 1) * 256], m[:], [m], [self.modscr.p(l)], m)
        k.barrier()

    def alloc_x(self):
        self.RX.reset()
        self.XT = [self.RX.alloc("xt%d" % i, [128, D], F32) for i in range(2)]
        self.MOD1 = self.RX.alloc("mod1", [128, D], F32)
        self.MOD2 = self.RX.alloc("mod2", [128, D], F32)
        self.HB = self.RX.alloc("hb", [128, D], BF16)
        self.xi = 0

    def modrow(self, l, idx, R):
        if R == 128:
            return self.modscr[l, self.NS, idx * D:(idx + 1) * D].partition_broadcast(128)
        return self.modscr[l, 0:R, idx * D:(idx + 1) * D]

    def stageA(self, l, which, xrows, nsub, R):
        k = self.k
        self.alloc_x()
        M1, M2 = self.MOD1, self.MOD2
        g = self.XT[1]
        k.dma("sp", M1[0:R, :], self.modrow(l, 1 + 3 * which, R), [self.modscr.p(l)], [M1], M1)
        k.dma("sp", M2[0:R, :], self.modrow(l, 0 + 3 * which, R), [self.modscr.p(l)], [M2], M2)
        k.dma("sp", g[0:R, :], self.norm_gain[l, 2 * which, :].partition_broadcast(R), [], [g], g)
        k.stt(M1[0:R, :], M1[0:R, :], 1.0, g[0:R, :], ALU.add, ALU.mult, [M1, g], [M1])
        self.cut(1)
        ss = self.small[0:R, 8:8 + nsub]
        k.memset("dve", ss, 0.0, [self.small.p(1)])
        for s in range(nsub):
            xt = self.XT[self.xi % 2]
            self.xi += 1
            part, ap = xrows(s)
            k.dma("sp", xt[0:R, :], ap, [part], [xt], xt)
            ssc = self.small[0:R, 8 + s:9 + s]
            k.act(self.HB[0:R, :], xt[0:R, :], AF.Square, [xt], [self.HB, self.small.p(1)], accum_out=ssc)
            self.cut(2)
            self.rstd_col(ssc, 1.0 / D)
            self.cut(3)
            k.stt(xt[0:R, :], xt[0:R, :], ssc, M1[0:R, :], ALU.mult, ALU.mult, [xt, self.small.p(1), M1], [xt])
            k.tt("dve", self.HB[0:R, :], xt[0:R, :], M2[0:R, :], ALU.add, [xt, M2], [self.HB])
            self.cut(4)
            for half in range(2):
                tp = self.TPB[self.tpi % 2]
                self.tpi += 1
                for j in range(8):
                    kc = half * 8 + j
                    k.tr(tp[:, j * 128:j * 128 + R], self.HB[0:R, kc * 128:(kc + 1) * 128], self.identb[0:R, 0:R],
                         [self.HB, self.identb], [tp])
                self.cut(5)
                src = tp[:, :].rearrange("p (a b) -> p a b", b=128)[:, :, 0:R]
                k.copy("act", self.hT[:, half * 8:half * 8 + 8, s * 128:s * 128 + R], src, [tp], [self.hT.p(s)])

    def stageC(self, l, which, xsrc, xdst, nsub, R, outbuf):
        k = self.k
        self.alloc_x()
        M1, M2 = self.MOD1, self.MOD2
        k.dma("sp", M1[0:R, :], self.modrow(l, 2 + 3 * which, R), [self.modscr.p(l)], [M1], M1)
        k.dma("sp", M2[0:R, :], self.norm_gain[l, 1 + 2 * which, :].partition_broadcast(R), [], [M2], M2)
        k.tt("dve", M1[0:R, :], M1[0:R, :], M2[0:R, :], ALU.mult, [M1, M2], [M1])
        if R == 128:
            self.dump("gate", M1, M1[:, :], [128, D])
        ss = self.small[0:R, 8:8 + nsub]
        k.memset("dve", ss, 0.0, [self.small.p(1)])
        for s in range(nsub):
            xt = self.XT[self.xi % 2]
            self.xi += 1
            ob = outbuf[0:R, s, :]
            part, ap = xsrc(s)
            k.dma("sp", xt[0:R, :], ap, [part], [xt], xt)
            ssc = self.small[0:R, 8 + s:9 + s]
            k.act(self.HB[0:R, :], ob, AF.Square, [outbuf.p(s)], [self.HB, self.small.p(1)], accum_out=ssc)
            self.rstd_col(ssc, 1.0 / D)
            if R == 128 and s == 3:
                self.dump("rs", self.small, self.small[:, 8:12], [128, 4])
            k.stt(ob, ob, ssc, M1[0:R, :], ALU.mult, ALU.mult, [outbuf.p(s), self.small.p(1), M1], [outbuf.p(s)])
            k.tt("pool", xt[0:R, :], xt[0:R, :], ob, ALU.add, [xt, outbuf.p(s)], [xt])
            dpart, dap = xdst(s)
            k.dma("sp", dap, xt[0:R, :], [xt], [dpart], xt)

    def out_proj(self, wt, l, nk_total, actT, nsub, R, outbuf, accumulate=False, row0=0):
        k = self.k
        npieces = nk_total // 16
        for cb in range(D // 256):
            banks = [self.mm_bank() for _ in range((nsub + 1) // 2)]
            assert npieces <= NW - 1
            ws = [self.piece([self.wspec(wt, l, row0 + pi * 2048, 16, cb * 256, 256)]) for pi in range(npieces)]
            for s in range(nsub):
                bank = banks[s // 2]
                c0 = (s % 2) * 256
                for pi in range(npieces):
                    w = ws[pi]
                    for kc in range(16):
                        kg = pi * 16 + kc
                        k.mm(bank[0:R, c0:c0 + 256], actT[:, kg, s * 128:s * 128 + R], w[:, kc, :],
                             pi == 0 and kc == 0, pi == npieces - 1 and kc == 15,
                             [w, actT], [bank])
            for s in range(nsub):
                bank = banks[s // 2]
                c0 = (s % 2) * 256
                dst = outbuf[0:R, s, cb * 256:(cb + 1) * 256]
                if accumulate:
                    k.tt("dve", dst, dst, bank[0:R, c0:c0 + 256], ALU.add,
                         [bank.p(2 * (s % 2), 2 * (s % 2) + 1), outbuf.p(s)], [outbuf.p(s)])
                else:
                    k.copy("act", dst, bank[0:R, c0:c0 + 256], [bank.p(2 * (s % 2), 2 * (s % 2) + 1)], [outbuf.p(s)])

    def fm_chunks(self, w, nchunk, nk, ncols, consume):
        k = self.k
        for j in range(nchunk):
            bank = self.mm_bank()
            for kc in range(nk):
                k.mm(bank[:, 0:ncols], w[:, kc, j * 128:(j + 1) * 128], self.hT[:, kc, 0:ncols], kc == 0, kc == nk - 1,
                     [w, self.hT], [bank])
            consume(j, bank)

    def mlp(self, l, nsub, R, ncols):
        k = self.k
        self.RO.reset()
        self.RA.reset()
        outbuf = self.RO.alloc("outbuf", [128, 4, D], F32, 4)
        uT = self.RA.alloc("uT", [128, 32, 512], BF16, 32)
        self.RX.reset()
        tmp = [self.RX.alloc("rl%d" % i, [128, 512], F32) for i in range(2)]
        ti = 0
        for hf in range(2):
            for pc in range(16):
                c0 = hf * 4096 + pc * 256
                w = self.piece([self.wspec(self.w_up, l, 0, 16, c0, 256)])

                def consume(j, bank, pc=pc):
                    nonlocal ti
                    t = tmp[ti % 2]
                    ti += 1
                    k.act(t[:, 0:ncols], bank[:, 0:ncols], AF.Relu, [bank], [t])
                    k.tt("dve", uT[:, pc * 2 + j, 0:ncols], t[:, 0:ncols], t[:, 0:ncols], ALU.mult, [t], [uT.p(pc * 2 + j)])
                self.fm_chunks(w, 2, 16, ncols, consume)
            self.out_proj(self.w_down, l, 32, uT, nsub, R, outbuf, accumulate=(hf == 1), row0=hf * 4096)
        k.barrier()
        return outbuf

    def sc_mixer(self, l, j_layer, nsub, R, ncols, sample, first, last):
        k = self.k
        jl = j_layer
        self.RO.reset()
        self.RA.reset()
        self.RX.reset()
        outbuf = self.RO.alloc("outbuf", [128, 4, D], F32, 4)
        gT = self.RA.alloc("gT", [128, 16, 512], BF16, 16)
        csb = [self.RX.alloc("csb%d" % i, [128, 512], F32) for i in range(2)]
        uraw = [self.RX.alloc("uraw%d" % i, [128, 514], F32) for i in range(2)]
        acc = [self.RX.alloc("acc%d" % i, [128, 512], F32) for i in range(2)]
        if sample:
            NS = self.NS
            bufsb = self.RX.alloc("bufsb", [2 * NS, D], F32)
            bufT = self.RX.alloc("bufT", [128, 16, 2 * NS], F32)
            unT = self.RX.alloc("unT", [128, 16, NS], F32, 16)
            unew = self.RX.alloc("unew", [NS, D], F32)
            k.dma("sp", bufsb[:], self.ssc[jl], [], [bufsb], bufsb)
            for c in range(16):
                dp, ps = self.dr_slot()
                k.tr(ps[:, 0:2 * NS], bufsb[:, c * 128:(c + 1) * 128], self.ident[0:2 * NS, 0:2 * NS], [bufsb, self.cst], [dp])
                k.copy("dve", bufT[:, c, :], ps[:, 0:2 * NS], [dp], [bufT])
        it = 0
        for jp in range(8):
            wb = self.piece([self.wspec(self.sc_w_in, jl, 0, 16, jp * 256, 256)])
            wc = self.piece([self.wspec(self.sc_w_in, jl, 0, 16, 2048 + jp * 256, 256)])
            wx = self.piece([self.wspec(self.sc_w_in, jl, 0, 16, 4096 + jp * 256, 256)])
            for jj in range(2):
                c = jp * 2 + jj
                cs, ur, ac = csb[it % 2], uraw[it % 2], acc[it % 2]
                it += 1
                bank = self.mm_bank()
                for kc in range(16):
                    k.mm(bank[:, 0:ncols], wc[:, kc, jj * 128:(jj + 1) * 128], self.hT[:, kc, 0:ncols], kc == 0, kc == 15, [wc, self.hT], [bank])
                k.copy("act", cs[:, 0:ncols], bank[:, 0:ncols], [bank], [cs])
                bank = self.mm_bank()
                for kc in range(16):
                    k.mm(bank[:, 0:ncols], wx[:, kc, jj * 128:(jj + 1) * 128], self.hT[:, kc, 0:ncols], kc == 0, kc == 15, [wx, self.hT], [bank])
                if not sample:
                    k.tt("dve", ur[:, 2:2 + ncols], cs[:, 0:ncols], bank[:, 0:ncols], ALU.mult, [cs, bank], [ur])
                    if first:
                        k.memset("pool", ur[:, 0:2], 0.0, [ur])
                    else:
                        k.copy("pool", ur[:, 0:2], self.sctail[:, c, :], [self.sctail.p(c)], [ur])
                    k.copy("pool", self.sctail[:, c, :], ur[:, ncols:ncols + 2], [ur], [self.sctail.p(c)])
                    k.ts("dve", ac[:, 0:ncols], ur[:, 0:ncols], self.scw[:, c, 0:1], None, ALU.mult, None, [ur, self.scw], [ac])
                    for t in (1, 2):
                        k.stt(ac[:, 0:ncols], ur[:, t:t + ncols], self.scw[:, c, t:t + 1], ac[:, 0:ncols], ALU.mult, ALU.add, [ur, self.scw, ac], [ac])
                else:
                    NS = self.NS
                    k.tt("dve", unT[:, c, :], cs[:, 0:ncols], bank[:, 0:ncols], ALU.mult, [cs, bank], [unT.p(c)])
                    bv = bufT[:, c, :].rearrange("p (b j) -> p b j", j=2)
                    k.ts("dve", ac[:, 0:NS], bv[:, :, 0], self.scw[:, c, 0:1], None, ALU.mult, None, [bufT, self.scw], [ac])
                    k.stt(ac[:, 0:NS], bv[:, :, 1], self.scw[:, c, 1:2], ac[:, 0:NS], ALU.mult, ALU.add, [bufT, self.scw, ac], [ac])
                    k.stt(ac[:, 0:NS], unT[:, c, :], self.scw[:, c, 2:3], ac[:, 0:NS], ALU.mult, ALU.add, [unT.p(c), self.scw, ac], [ac])
                bank = self.mm_bank()
                for kc in range(16):
                    k.mm(bank[:, 0:ncols], wb[:, kc, jj * 128:(jj + 1) * 128], self.hT[:, kc, 0:ncols], kc == 0, kc == 15, [wb, self.hT], [bank])
                k.tt("dve", gT[:, c, 0:ncols], ac[:, 0:ncols], bank[:, 0:ncols], ALU.mult, [ac, bank], [gT.p(c)])
        if sample:
            NS = self.NS
            for c in range(16):
                dp, ps = self.dr_slot()
                k.tr(ps[0:NS, :], unT[:, c, :], self.ident, [unT.p(c), self.cst], [dp])
                k.copy("act", unew[:, c * 128:(c + 1) * 128], ps[0:NS, :], [dp], [unew])
            dst = self.nss[jl].rearrange("(b j) d -> b j d", j=2)
            src = self.ssc[jl].rearrange("(b j) d -> b j d", j=2)
            k.dma("sp", dst[:, 1, :], unew[:], [unew], [self.nss], unew)
            k.dma("sp", dst[:, 0, :], src[:, 1, :], [], [self.nss], unew)
        elif last:
            tl = self.RX.alloc("tl", [2, D], F32)
            for c in range(16):
                dp, ps = self.dr_slot()
                k.tr(ps[0:2, :], self.sctail[:, c, :], self.ident, [self.sctail.p(c), self.cst], [dp])
                k.copy("act", tl[:, c * 128:(c + 1) * 128], ps[0:2, :], [dp], [tl])
            k.dma("sp", self.nsp[jl], tl[:], [tl], [self.nsp], tl)
        self.out_proj(self.sc_w_out, jl, 16, gT, nsub, R, outbuf)
        k.barrier()
        return outbuf

    def layer_consts(self, l):
        k = self.k
        self.RX.reset()
        if l % 2 == 0:
            jl = l // 2
            lay = self.lay
            k.dma("sp", lay[:, 0:192], self.gsm[jl, 0:192].partition_broadcast(128), [], [lay], lay)
            self.cut(9)
            cwr = self.RX.alloc("cwr", [32, CONVD], F32)
            k.memset("pool", cwr[:], 0.0, [cwr])
            k.dma("sp", cwr[0:4, :], self.g_conv_w[jl], [], [cwr], cwr)
            for c in range(64):
                dp, ps = self.dr_slot()
                k.tr(ps[:, 0:32], cwr[:, c * 128:(c + 1) * 128], self.ident[0:32, 0:32], [cwr, self.cst], [dp])
                k.copy("dve", self.cw[:, c, :], ps[:, 0:4], [dp], [self.cw])
            self.cut(10)
            lay = self.lay
            self.cut(12)
            k.act(lay[:, NH:2 * NH], lay[:, NH:2 * NH], AF.Exp, [lay], [lay])
            self.cut(13)
            k.ts("dve", lay[:, NH:2 * NH], lay[:, NH:2 * NH], -1.0, None, ALU.mult, None, [lay], [lay])
            self.cut(11)
        else:
            jl = l // 2
            cwr = self.RX.alloc("cwr", [32, D], F32)
            k.memset("pool", cwr[:], 0.0, [cwr])
            k.dma("sp", cwr[0:3, :], self.sc_conv_w[jl], [], [cwr], cwr)
            for c in range(16):
                dp, ps = self.dr_slot()
                k.tr(ps[:, 0:32], cwr[:, c * 128:(c + 1) * 128], self.ident[0:32, 0:32], [cwr, self.cst], [dp])
                k.copy("dve", self.scw[:, c, :], ps[:, 0:3], [dp], [self.scw])
        k.barrier()

    def xrows_fn(self, src_tile, row0, R, part_fn):
        def f(s):
            return part_fn(s), src_tile[row0 + s * 128:row0 + s * 128 + R, :]
        return f

    def dump(self, name, tile, ap, shape):
        import os
        if not os.environ.get("DUMP"):
            return
        if name in self.dbg:
            return
        d = self.k.dram("dbg_" + name, list(shape), F32, "ExternalOutput")
        self.dbg[name] = d
        self.k.dma("pool", d[:], ap, [tile], [d], tile)

    def cut(self, n):
        import os
        if os.environ.get("CUT") == str(n):
            self.k.barrier()
            raise StopIteration

    def ck(self):
        self.stage += 1
        if self.stop_after is not None and self.stage >= self.stop_after:
            self.k.barrier()
            raise StopIteration

    def build(self, stop_after):
        k = self.k
        NP, NS, DEPTH = self.NP, self.NS, self.DEPTH
        self.preconvert(0)
        self.adaln()
        self.ck()
        import os
        nomlp = bool(os.environ.get("NOMLP"))
        subs = [(l_, w_) for l_ in range(DEPTH) for w_ in range(2) if not (nomlp and w_ == 1)]
        ntile = NP // 512
        for l in range(DEPTH):
            self.layer_consts(l)
            for which in range(2):
                if (l, which) not in subs:
                    continue
                is_first = (l, which) == subs[0]
                is_last = (l, which) == subs[-1]
                for ps_ in range(ntile + 1):
                    sample = ps_ == ntile
                    if sample:
                        nsub, R, ncols, row0 = 1, NS, NS, NP
                        src_in, dst_out = self.xs, self.ys
                        in0, out0 = 0, 0
                    else:
                        nsub, R, ncols, row0 = 4, 128, 512, ps_ * 512
                        src_in, dst_out = self.xp, self.yp
                        in0, out0 = row0, row0
                    sp0 = row0 // 128

                    def scr_part(s, sp0=sp0):
                        return self.xscr.p(sp0 + s)

                    def io_part(t):
                        return lambda s: t
                    if is_first:
                        xsrc = self.xrows_fn(src_in, in0, R, io_part(src_in))
                    else:
                        xsrc = self.xrows_fn(self.xscr, row0, R, scr_part)
                    if is_last:
                        xdst = self.xrows_fn(dst_out, out0, R, io_part(dst_out))
                    else:
                        xdst = self.xrows_fn(self.xscr, row0, R, scr_part)
                    self.stageA(l, which, xsrc, nsub, R)
                    k.barrier()
                    self.ck()
                    if which == 1:
                        outbuf = self.mlp(l, nsub, R, ncols)
                    elif l % 2 == 1:
                        outbuf = self.sc_mixer(l, l // 2, nsub, R, ncols, sample, ps_ == 0, ps_ == ntile - 1)
                    else:
                        outbuf = self.gdn_mixer(l, l // 2, nsub, R, ncols, sample, ps_ == 0, ps_ == ntile - 1)
                    self.ck()
                    self.stageC(l, which, xsrc, xdst, nsub, R, outbuf)
                    k.barrier()
                    self.ck()

    def gdn_mixer(self, l, jl, nsub, R, ncols, sample, first, last):
        k = self.k
        self.RO.reset()
        self.RA.reset()
        self.RX.reset()
        if sample:
            return self.gdn_sample(l, jl)
        NHh = NH
        oT = self.RA.alloc("oT", [128, 32, 512], BF16, 32)
        tok = self.tok
        lay = self.lay
        if first:
            k.memset("pool", self.S[:, :, :], 0.0, [self.S])
        w = self.piece([self.wspec(self.w_ba, jl, 0, 16, 0, 64)])
        bank = self.mm_bank()
        for s in range(4):
            for kc in range(16):
                k.mm(bank[:, s * 64:(s + 1) * 64], self.hT[:, kc, s * 128:(s + 1) * 128], w[:, kc, 0:64], kc == 0, kc == 15,
                     [w, self.hT], [bank])
        bview = bank[:, 0:256].rearrange("p (s x) -> p s x", x=64)
        T = lambda i: tok[:, i]
        TP = lambda i: tok.p(i)
        k.act(T(0), bview[:, :, 0:32], AF.Exp, [bank], [TP(0)], scale=-1.0)
        k.act(T(0), T(0), AF.Ln, [TP(0)], [TP(0)], bias=1.0)
        k.ts("dve", T(0), T(0), -0.5, None, ALU.mult, None, [TP(0)], [TP(0)])
        for s in range(4):
            k.tt("dve", tok[:, 1, s, :], bview[:, s, 32:64], lay[:, 0:NHh], ALU.add, [bank, lay], [TP(1)])
        k.act(T(1), T(1), AF.Exp, [TP(1)], [TP(1)])
        k.act(T(1), T(1), AF.Ln, [TP(1)], [TP(1)], bias=1.0)
        for s in range(4):
            k.tt("dve", tok[:, 1, s, :], tok[:, 1, s, :], lay[:, NHh:2 * NHh], ALU.mult, [TP(1), lay], [TP(1)])
        for s in range(4):
            dp, ps = self.dr_slot()
            k.mm(ps[:, 0:32], self.Ui, tok[:, 1, s, :], True, True, [self.cst, TP(1)], [dp])
            k.copy("dve", tok[:, 2, s, :], ps[:, 0:32], [dp], [TP(2)])
        for s in range(4):
            dp, ps = self.dr_slot()
            k.mm(ps[:, 0:32], self.SelLast, tok[:, 2, s, :], True, True, [self.cst, TP(2)], [dp])
            k.copy("dve", tok[:, 4, s, :], ps[:, 0:32], [dp], [TP(4)])
        k.ts("dve", T(3), T(2), -1.0, None, ALU.mult, None, [TP(2)], [TP(3)])
        k.act(T(5), T(2), AF.Exp, [TP(2)], [TP(5)])
        k.tt("dve", T(10), T(0), T(2), ALU.add, [TP(0), TP(2)], [TP(10)])
        k.act(T(6), T(10), AF.Exp, [TP(10)], [TP(6)])
        k.ts("dve", T(6), T(6), -1.0, None, ALU.mult, None, [TP(6)], [TP(6)])
        k.act(T(7), T(0), AF.Exp, [TP(0)], [TP(7)])
        k.tt("dve", T(10), T(4), T(3), ALU.add, [TP(4), TP(3), TP(10)], [TP(10)])
        k.tt("dve", T(10), T(10), T(0), ALU.add, [TP(10), TP(0)], [TP(10)])
        k.act(T(8), T(10), AF.Exp, [TP(10)], [TP(8)])
        k.act(T(9), T(4), AF.Exp, [TP(4)], [TP(9)])

        RO, RX = self.RO, self.RX
        raw = RO.alloc("raw", [128, 4, 515], F32, 4)
        cvb = [RO.alloc("cvb%d" % i, [128, 512], F32) for i in range(2)]
        sq = RO.alloc("sq", [128, 512], BF16)
        rinv = RO.alloc("rinv", [128, 512], F32)
        QnT = RO.alloc("QnT", [128, 512], BF16)
        KnT = RO.alloc("KnT", [128, 512], BF16)
        vT = RO.alloc("vT", [128, 2, 512], BF16, 2)
        Vs = RO.alloc("Vs", [128, 4, 2, 128], F32, 8)
        zg = RO.alloc("zg", [128, 4, 256], F32, 4)
        Kd = RO.alloc("Kd", [128, 4, 2, 128], BF16, 8)
        XZ = [[RX.alloc("xz%d_%d" % (u, i), [128, 128], F32) for i in range(4)] for u in range(4)]
        PP = [[RX.alloc("pp%d_%d" % (u, i), [128, 128], F32) for i in range(2)] for u in range(4)]
        TT_ = [[RX.alloc("tt%d_%d" % (u, i), [128, 128], F32) for i in range(3)] for u in range(4)]
        DG = [RX.alloc("dg%d" % u, [128, 256], F32) for u in range(4)]
        PT = [[RX.alloc("pt%d_%d" % (j, c), [128, 128], BF16) for c in range(4)] for j in range(2)]
        QK = [[RX.alloc("qk%d_%d" % (j, c), [128, 128], BF16) for c in range(4)] for j in range(2)]
        Rb = [RX.alloc("rb%d" % j, [128, 128], BF16) for j in range(2)]
        Yb = [RX.alloc("yb%d" % j, [128, 128], BF16) for j in range(2)]
        ogb = [RX.alloc("og%d" % j, [128, 128], BF16) for j in range(2)]
        Of = [RX.alloc("of%d" % j, [128, 128], F32) for j in range(2)]
        tO = [RX.alloc("to%d" % j, [128, 128], F32) for j in range(2)]
        Sbf = RX.alloc("sbf", [128, 2, 128], BF16, 2)
        onesb = RX.alloc("onesb", [128, 128], BF16)
        k.copy("dve", onesb[:], self.ones, [self.cst], [onesb])
        ci = 0
        for hk in range(NKH):
            wqk = self.piece([self.wspec(self.w_qkvz, jl, 0, 16, hk * 128, 128),
                              self.wspec(self.w_qkvz, jl, 0, 16, 2048 + hk * 128, 128)])
            wv = self.piece([self.wspec(self.w_qkvz, jl, 0, 16, 4096 + hk * 256, 256)])
            wz = self.piece([self.wspec(self.w_qkvz, jl, 0, 16, 8192 + hk * 256, 256)])
            for j in range(2):
                k.copy("act", Sbf[:, j, :], self.S[:, 2 * hk + j, :], [self.S.p(2 * hk + j)], [Sbf.p(j)])
            for r in range(4):
                wt, jj = (wqk, r) if r < 2 else (wv, r - 2)
                gch = (hk, 16 + hk, 32 + 2 * hk, 33 + 2 * hk)[r]
                bank = self.mm_bank()
                for kc in range(16):
                    k.mm(bank[:, 0:512], wt[:, kc, jj * 128:(jj + 1) * 128], self.hT[:, kc, 0:512], kc == 0, kc == 15, [wt, self.hT], [bank])
                k.copy("act", raw[:, r, 3:515], bank[:, 0:512], [bank], [raw.p(r)])
                if first:
                    k.memset("pool", raw[:, r, 0:3], 0.0, [raw.p(r)])
                else:
                    k.copy("pool", raw[:, r, 0:3], self.qtail[:, gch, :], [self.qtail.p(gch)], [raw.p(r)])
                k.copy("pool", self.qtail[:, gch, :], raw[:, r, 512:515], [raw.p(r)], [self.qtail.p(gch)])
                cv = cvb[ci % 2]
                ci += 1
                k.ts("dve", cv[:], raw[:, r, 0:512], self.cw[:, gch, 0:1], None, ALU.mult, None, [raw.p(r), self.cw], [cv])
                for t in (1, 2, 3):
                    k.stt(cv[:], raw[:, r, t:t + 512], self.cw[:, gch, t:t + 1], cv[:], ALU.mult, ALU.add, [raw.p(r), self.cw, cv], [cv])
                k.act(cv[:], cv[:], AF.Silu, [cv], [cv])
                if r < 2:
                    k.act(sq[:], cv[:], AF.Square, [cv], [sq])
                    b2 = self.mm_bank()
                    k.mm(b2[:, 0:512], onesb[:], sq[:], True, True, [onesb, sq], [b2])
                    k.act(rinv[:], b2[:, 0:512], AF.Sqrt, [b2, self.small], [rinv], bias=self.epsc, scale=1.0)
                    k.recip(rinv[:], rinv[:], [rinv], [rinv])
                    if r == 0:
                        k.stt(QnT[:], cv[:], float(128 ** -0.5), rinv[:], ALU.mult, ALU.mult, [cv, rinv], [QnT])
                    else:
                        k.tt("dve", KnT[:], cv[:], rinv[:], ALU.mult, [cv, rinv], [KnT])
                else:
                    j = r - 2
                    h = 2 * hk + j
                    k.copy("act", vT[:, j, :], cv[:], [cv], [vT.p(j)])
                    for c in range(4):
                        tpp, tps = self.tp_slot()
                        k.tr(tps, vT[:, j, c * 128:(c + 1) * 128], self.identb[:], [vT.p(j), self.identb], [tpp])
                        k.act(Vs[:, c, j, :], tps, AF.Identity, [tpp, TP(7)], [Vs.p(c * 2 + j)], scale=tok[:, 7, c, h:h + 1])
            zb = [self.mm_bank(), self.mm_bank()]
            for s in range(4):
                bank = zb[s // 2]
                c0 = (s % 2) * 256
                for kc in range(16):
                    k.mm(bank[:, c0:c0 + 256], self.hT[:, kc, s * 128:(s + 1) * 128], wz[:, kc, :], kc == 0, kc == 15,
                         [wz, self.hT], [bank.p(2 * (s % 2), 2 * (s % 2) + 1)])
            for s in range(4):
                bank = zb[s // 2]
                c0 = (s % 2) * 256
                k.act(zg[:, s, :], bank[:, c0:c0 + 256], AF.Silu, [bank.p(2 * (s % 2), 2 * (s % 2) + 1)], [zg.p(s)])
                for j in range(2):
                    k.tt("pool", zg[:, s, j * 128:(j + 1) * 128], zg[:, s, j * 128:(j + 1) * 128], lay[:, 64:192], ALU.mult, [zg.p(s), lay], [zg.p(s)])
            for c in range(4):
                tpp, tps = self.tp_slot()
                k.tr(tps, KnT[:, c * 128:(c + 1) * 128], self.identb[:], [KnT, self.identb], [tpp])
                for j in range(2):
                    h = 2 * hk + j
                    k.act(Kd[:, c, j, :], tps, AF.Identity, [tpp, TP(8)], [Kd.p(c * 2 + j)], scale=tok[:, 8, c, h:h + 1])
            for j in range(2):
                h = 2 * hk + j
                cur = [0, 0, 0, 0]
                for c in range(4):
                    u = c
                    dg, (t1, t2, xx) = DG[u], TT_[u]
                    k.ts("dve", dg[:, 0:128], self.ident, tok[:, 2, c, h:h + 1], None, ALU.mult, None, [self.cst, TP(2)], [dg])
                    k.ts("dve", dg[:, 128:256], self.ident, tok[:, 0, c, h:h + 1], None, ALU.mult, None, [self.cst, TP(0)], [dg])
                    g1p, g1 = self.dr_slot()
                    k.mm(g1, self.ones, dg[:, 0:128], True, True, [self.cst, dg], [g1p])
                    g2p, g2 = self.dr_slot()
                    k.mm(g2, self.ones, dg[:, 128:256], True, True, [self.cst, dg], [g2p])
                    k.act(t1[:], g1, AF.Abs, [g1p, TP(3)], [t1], bias=tok[:, 3, c, h:h + 1], scale=1.0)
                    k.stt(t2[:], g2, tok[:, 0, c, h:h + 1], t1[:], ALU.add, ALU.subtract, [g2p, TP(0), t1], [t2])
                    k.act(t2[:], t2[:], AF.Exp, [t2], [t2])
                    k.act(t1[:], t1[:], AF.Exp, [t1, TP(0)], [t1], bias=tok[:, 0, c, h:h + 1], scale=-1.0)
                    kkp, kk = self.dr_slot()
                    k.mm(kk, KnT[:, c * 128:(c + 1) * 128], KnT[:, c * 128:(c + 1) * 128], True, True, [KnT], [kkp])
                    kqp, kq = self.dr_slot()
                    k.mm(kq, KnT[:, c * 128:(c + 1) * 128], QnT[:, c * 128:(c + 1) * 128], True, True, [KnT, QnT], [kqp])
                    k.tt("dve", xx[:], kk, t2[:], ALU.mult, [kkp, t2], [xx])
                    X0, Z0 = XZ[u][0], XZ[u][1]
                    k.tt("dve", X0[:], xx[:], self.negLs, ALU.mult, [xx, self.cst], [X0])
                    k.tt("pool", Z0[:], xx[:], self.negUs, ALU.mult, [xx, self.cst], [Z0])
                    k.tt("pool", PP[u][0][:], Z0[:], self.ident, ALU.add, [Z0, self.cst], [PP[u][0]])
                    k.tt("dve", t2[:], kq, t1[:], ALU.mult, [kqp, t1, t2], [t2])
                    k.tt("pool", QK[j][c][:], t2[:], self.Ui, ALU.mult, [t2, self.cst], [QK[j][c]])
                for lev in range(6):
                    for c in range(4):
                        u = c
                        a = cur[u]
                        Xk, Zk = XZ[u][2 * a], XZ[u][2 * a + 1]
                        Xn, Zn = XZ[u][2 * (1 - a)], XZ[u][2 * (1 - a) + 1]
                        pa = lev % 2
                        Pk, Pn = PP[u][pa], PP[u][1 - pa]
                        xp_, xps = self.dr_slot()
                        k.mm(xps, Zk[:], Xk[:], True, True, [Zk, Xk], [xp_])
                        if lev < 5:
                            zp_, zps = self.dr_slot()
                            k.mm(zps, Xk[:], Zk[:], True, True, [Zk, Xk], [zp_])
                        k.copy("act", Xn[:], xps, [xp_], [Xn])
                        if lev < 5:
                            k.copy("dve", Zn[:], zps, [zp_], [Zn])
                        pp_, pps = self.dr_slot()
                        k.mm(pps, Xn[:], Pk[:], True, True, [Xn, Pk], [pp_])
                        if lev < 5:
                            k.tt("dve", Pn[:], Pk[:], pps, ALU.add, [Pk, pp_], [Pn])
                        else:
                            k.tt("dve", PT[j][c][:], Pk[:], pps, ALU.add, [Pk, pp_], [PT[j][c]])
                        cur[u] = 1 - a
            for c in range(4):
                cs = slice(c * 128, (c + 1) * 128)
                for j in range(2):
                    h = 2 * hk + j
                    ksp, ks = self.dr_slot()
                    k.mm(ks, KnT[:, cs], Sbf[:, j, :], True, True, [KnT, Sbf.p(j)], [ksp])
                    k.stt(Rb[j][:], ks, tok[:, 6, c, h:h + 1], Vs[:, c, j, :], ALU.mult, ALU.add, [ksp, TP(6), Vs.p(c * 2 + j)], [Rb[j]])
                    yp_, ys_ = self.dr_slot()
                    k.mm(ys_, PT[j][c][:], Rb[j][:], True, True, [PT[j][c], Rb[j]], [yp_])
                    k.copy("act", Yb[j][:], ys_, [yp_], [Yb[j]])
                    qsp, qs = self.dr_slot()
                    k.mm(qs, QnT[:, cs], Sbf[:, j, :], True, True, [QnT, Sbf.p(j)], [qsp])
                    oip, oi = self.dr_slot()
                    k.mm(oi, QK[j][c][:], Yb[j][:], True, True, [QK[j][c], Yb[j]], [oip])
                    k.act(tO[j][:], qs, AF.Identity, [qsp, TP(5)], [tO[j]], scale=tok[:, 5, c, h:h + 1])
                    k.tt("dve", Of[j][:], oi, tO[j][:], ALU.add, [oip, tO[j]], [Of[j]])
                    ssc = self.small[:, 16 + j:17 + j]
                    k.memset("pool", ssc, 0.0, [self.small.p(2 + j)])
                    k.act(tO[j][:], Of[j][:], AF.Square, [Of[j]], [tO[j], self.small.p(2 + j)], accum_out=ssc)
                    k.act(ssc, ssc, AF.Sqrt, [self.small.p(0), self.small.p(2 + j)], [self.small.p(2 + j)], bias=self.epsc, scale=1.0 / 128)
                    k.recip(ssc, ssc, [self.small.p(2 + j)], [self.small.p(2 + j)])
                    k.stt(ogb[j][:], Of[j][:], ssc, zg[:, c, j * 128:(j + 1) * 128], ALU.mult, ALU.mult, [Of[j], self.small.p(2 + j), zg.p(c)], [ogb[j]])
                    if hk == 0 and c == 0 and j == 0:
                        self.dump("Of", Of[j], Of[j][:], [128, 128])
                        self.dump("ssc", self.small, self.small[:, 16:18], [128, 2])
                        self.dump("zg", zg, zg[:, 0, 0:128], [128, 128])
                        self.dump("ogb", ogb[j], ogb[j][:], [128, 128])
                        self.dump("Yb", Yb[j], Yb[j][:], [128, 128])
                        self.dump("lay", self.lay, self.lay[:, :], [128, 192])
                    tpp, tps = self.tp_slot()
                    k.tr(tps, ogb[j][:], self.identb[:], [ogb[j], self.identb], [tpp])
                    k.copy("act", oT[:, h, cs], tps, [tpp], [oT.p(h)])
                    sup, su = self.dr_slot()
                    k.mm(su, Kd[:, c, j, :], Yb[j][:], True, True, [Kd.p(c * 2 + j), Yb[j]], [sup])
                    k.stt(self.S[:, h, :], self.S[:, h, :], tok[:, 9, c, h:h + 1], su, ALU.mult, ALU.add, [self.S.p(h), TP(9), sup], [self.S.p(h)])
                    k.copy("act", Sbf[:, j, :], self.S[:, h, :], [self.S.p(h)], [Sbf.p(j)])
        k.barrier()
        if last:
            k.dma("sp", self.ndp[jl].rearrange("h k v -> k h v"), self.S[:, :, :], [self.S], [self.ndp], self.S)
            self.RX.reset()
            tl = [self.RX.alloc("tl%d" % i, [3, 2048], F32) for i in range(2)]
            for q in range(4):
                t = tl[q % 2]
                for cc in range(16):
                    c = q * 16 + cc
                    dp, ps = self.dr_slot()
                    k.tr(ps[0:3, :], self.qtail[:, c, :], self.ident, [self.qtail.p(c), self.cst], [dp])
                    k.copy("act", t[:, cc * 128:(cc + 1) * 128], ps[0:3, :], [dp], [t])
                k.dma("sp", self.nqp[jl, :, q * 2048:(q + 1) * 2048], t[:], [t], [self.nqp], t)
            k.barrier()
        self.dump("oT", oT, oT[:, :, :], [128, 32, 512])
        self.RO.reset()
        outbuf = self.RO.alloc("outbuf", [128, 4, D], F32, 4)
        self.out_proj(self.g_w_out, jl, 32, oT, nsub, R, outbuf)
        k.barrier()
        self.dump("outbuf", outbuf, outbuf[:, :, :], [128, 4, D])
        k.barrier()
        return outbuf

    def gdn_sample(self, l, jl):
        k = self.k
        NS = self.NS
        RO, RX, RA = self.RO, self.RX, self.RA
        lay = self.lay
        oTs = RX.alloc("oTs", [128, 32, NS], BF16)
        rawT = RX.alloc("rawT", [128, 64, NS], F32, 64)
        zT = RX.alloc("zT", [128, 32, NS], F32, 32)
        qkvT = RX.alloc("qkvT", [128, 64, NS], F32, 64)
        kq = RX.alloc("kq", [128, NS, 16, 2], F32)
        BCb = RX.alloc("BCb", [128, NS, 32], F32)
        BCa = RX.alloc("BCa", [128, NS, 32], F32)
        KQb = RX.alloc("KQb", [128, NS, 16], F32)
        OB = RX.alloc("OB", [128, 32, NS], F32)
        tk = RX.alloc("tk", [NS, 4, 32], F32)
        big = RX.alloc("bigd", [NS, 512], F32)
        v1 = RX.alloc("v1", [128, 32], F32)
        v2 = RX.alloc("v2", [128, 32], F32)
        v3 = RX.alloc("v3", [128, 32], F32)
        sqs = RX.alloc("sqs", [128, 512], F32)
        dgs = [RX.alloc("dgs%d" % i, [128, 128], F32) for i in range(2)]
        acc = RX.alloc("accs", [128, NS], F32)
        Sb = [RO.alloc("Sb0", [128, 32, 128], F32, 32), RA.alloc("Sb1", [128, 32, 128], F32, 32)]
        rown = [RO.alloc("rown%d" % i, [3 * NS, 2048], F32) for i in range(2)]
        bufT = RA.alloc("bufT", [128, 64, 3 * NS], F32, 64)
        stg = [rown[0]]
        for q in range(4):
            st = stg[0]
            k.dma("sp", st[:], self.sqkv[jl, :, q * 2048:(q + 1) * 2048], [], [st], st)
            for cc in range(16):
                c = q * 16 + cc
                dp, ps = self.dr_slot()
                k.tr(ps[:, 0:3 * NS], st[:, cc * 128:(cc + 1) * 128], self.ident[0:3 * NS, 0:3 * NS], [st, self.cst], [dp])
                k.copy("dve", bufT[:, c, :], ps[:, 0:3 * NS], [dp], [bufT.p(c)])
        k.barrier()
        for hk in range(NKH):
            wqk = self.piece([self.wspec(self.w_qkvz, jl, 0, 16, hk * 128, 128),
                              self.wspec(self.w_qkvz, jl, 0, 16, 2048 + hk * 128, 128)])
            wv = self.piece([self.wspec(self.w_qkvz, jl, 0, 16, 4096 + hk * 256, 256)])
            wz = self.piece([self.wspec(self.w_qkvz, jl, 0, 16, 8192 + hk * 256, 256)])

            def craw(chs):
                def f(j, bank):
                    k.copy("act", rawT[:, chs[j], :], bank[:, 0:NS], [bank], [rawT.p(chs[j])])
                return f

            def cz(chs):
                def f(j, bank):
                    k.act(zT[:, chs[j], :], bank[:, 0:NS], AF.Silu, [bank], [zT.p(chs[j])])
                return f
            self.fm_chunks(wqk, 2, 16, NS, craw([hk, 16 + hk]))
            self.fm_chunks(wv, 2, 16, NS, craw([32 + 2 * hk, 33 + 2 * hk]))
            self.fm_chunks(wz, 2, 16, NS, cz([2 * hk, 2 * hk + 1]))
        w = self.piece([self.wspec(self.w_ba, jl, 0, 16, 0, 64)])
        bank = self.mm_bank()
        for kc in range(16):
            k.mm(bank[0:NS, 0:64], self.hT[:, kc, 0:NS], w[:, kc, 0:64], kc == 0, kc == 15, [w, self.hT], [bank])
        k.act(tk[:, 0, :], bank[0:NS, 0:32], AF.Exp, [bank], [tk], scale=-1.0)
        k.act(tk[:, 0, :], tk[:, 0, :], AF.Ln, [tk], [tk], bias=1.0)
        k.act(tk[:, 0, :], tk[:, 0, :], AF.Exp, [tk], [tk], scale=-1.0)
        k.tt("dve", tk[:, 1, :], bank[0:NS, 32:64], lay[0:NS, 0:NH], ALU.add, [bank, lay], [tk])
        k.act(tk[:, 1, :], tk[:, 1, :], AF.Exp, [tk], [tk])
        k.act(tk[:, 1, :], tk[:, 1, :], AF.Ln, [tk], [tk], bias=1.0)
        k.tt("dve", tk[:, 1, :], tk[:, 1, :], lay[0:NS, NH:2 * NH], ALU.mult, [tk, lay], [tk])
        k.act(tk[:, 1, :], tk[:, 1, :], AF.Exp, [tk], [tk])
        for qi, dst in ((0, BCb), (1, BCa)):
            bv = big[:, :].rearrange("p (b h) -> p b h", h=32)
            for b in range(NS):
                k.ts("dve", bv[:, b, :], tk[:, qi, :], self.ident[0:NS, b:b + 1], None, ALU.mult, None, [tk, self.cst, big], [big])
            bank = self.mm_bank()
            k.mm(bank[:, 0:512], self.ones[0:NS, :], big[:, :], True, True, [self.cst, big], [bank])
            k.copy("act", dst[:, :, :], bank[:, 0:512].rearrange("p (b h) -> p b h", h=32), [bank], [dst])
        for c in range(64):
            bv = bufT[:, c, :].rearrange("p (b j) -> p b j", j=3)
            k.ts("dve", acc[:], bv[:, :, 0], self.cw[:, c, 0:1], None, ALU.mult, None, [bufT.p(c), self.cw], [acc])
            k.stt(acc[:], bv[:, :, 1], self.cw[:, c, 1:2], acc[:], ALU.mult, ALU.add, [bufT.p(c), self.cw, acc], [acc])
            k.stt(acc[:], bv[:, :, 2], self.cw[:, c, 2:3], acc[:], ALU.mult, ALU.add, [bufT.p(c), self.cw, acc], [acc])
            k.stt(acc[:], rawT[:, c, :], self.cw[:, c, 3:4], acc[:], ALU.mult, ALU.add, [rawT.p(c), self.cw, acc], [acc])
            k.act(qkvT[:, c, :], acc[:], AF.Silu, [acc], [qkvT.p(c)])
        qk2 = qkvT[:, 0:32, :]
        k.tt("dve", sqs[:, :].rearrange("p (c b) -> p c b", b=NS), qk2, qk2, ALU.mult, [qkvT], [sqs])
        bank = self.mm_bank()
        k.mm(bank[:, 0:512], self.ones, sqs[:], True, True, [self.cst, sqs], [bank])
        k.act(sqs[:], bank[:, 0:512], AF.Sqrt, [bank, self.small], [sqs], bias=self.epsc, scale=1.0)
        k.recip(sqs[:], sqs[:], [sqs], [sqs])
        rv = sqs[:, :].rearrange("p (c b) -> p c b", b=NS)
        kqv_k = kq[:, :, :, 0].rearrange("p b h -> p h b")
        kqv_q = kq[:, :, :, 1].rearrange("p b h -> p h b")
        k.stt(kqv_q, qkvT[:, 0:16, :], float(128 ** -0.5), rv[:, 0:16, :], ALU.mult, ALU.mult, [qkvT, sqs], [kq])
        k.tt("dve", kqv_k, qkvT[:, 16:32, :], rv[:, 16:32, :], ALU.mult, [qkvT, sqs], [kq])
        sq2 = sqs[:, 0:NS * 16].rearrange("p (b h) -> p b h", h=16)
        k.tt("dve", sq2, kq[:, :, :, 0], kq[:, :, :, 1], ALU.mult, [kq, sqs], [sqs])
        bank = self.mm_bank()
        k.mm(bank[:, 0:NS * 16], self.ones, sqs[:, 0:NS * 16], True, True, [self.cst, sqs], [bank])
        k.copy("act", KQb[:, :, :], bank[:, 0:NS * 16].rearrange("p (b h) -> p b h", h=16), [bank], [KQb])
        k.barrier()
        dstv = self.nqs[jl].rearrange("(b j) d -> b j d", j=3)
        srcv = self.sqkv[jl].rearrange("(b j) d -> b j d", j=3)
        for q in range(4):
            rn = rown[q % 2]
            for cc in range(16):
                c = q * 16 + cc
                dp, ps = self.dr_slot()
                k.tr(ps[0:NS, :], rawT[:, c, :], self.ident, [rawT.p(c), self.cst], [dp])
                k.copy("act", rn[0:NS, cc * 128:(cc + 1) * 128], ps[0:NS, :], [dp], [rn])
            k.dma("sp", dstv[:, 2, q * 2048:(q + 1) * 2048], rn[0:NS, :], [rn], [self.nqs], rn)
        k.dma("sp", dstv[:, 0:2, :], srcv[:, 1:3, :], [], [self.nqs], rown[0])
        k.barrier()
        for b in range(NS):
            sb = Sb[b % 2]
            k.dma("sp", sb[:, :, :], self.sdel[jl, b].rearrange("h k v -> k h v"), [], [sb], sb)
            dp, ps = self.dr_slot()
            psv = ps[:, 0:64].rearrange("p (h t) -> p h t", t=2)
            for h in range(NH):
                k.mm(ps[:, 2 * h:2 * h + 2], sb[:, h, :], kq[:, b, h // 2, :], True, True, [sb.p(h), kq], [dp])
            k.mm(ps[:, 64:128], self.ident, self.ident[:, 0:64], True, True, [self.cst], [dp])
            k.tt("dve", v1[:], psv[:, :, 0], BCa[:, b, :], ALU.mult, [dp, BCa], [v1])
            k.tt("dve", v1[:], qkvT[:, 32:64, b], v1[:], ALU.subtract, [qkvT, v1], [v1])
            k.tt("dve", v1[:], v1[:], BCb[:, b, :], ALU.mult, [v1, BCb], [v1])
            k.tt("dve", v2[:], psv[:, :, 1], BCa[:, b, :], ALU.mult, [dp, BCa], [v2])
            v1v = v1[:, :].rearrange("p (h t) -> p h t", t=2)
            v3v = v3[:, :].rearrange("p (h t) -> p h t", t=2)
            for t in range(2):
                k.tt("dve", v3v[:, :, t], v1v[:, :, t], KQb[:, b, :], ALU.mult, [v1, KQb], [v3])
            k.tt("dve", OB[:, :, b], v3[:], v2[:], ALU.add, [v3, v2], [OB])
            for h in range(NH):
                dg = dgs[h % 2]
                k.ts("dve", dg[:], self.ident, v1[:, h:h + 1], None, ALU.mult, None, [self.cst, v1], [dg])
                vp, vps = self.dr_slot()
                k.mm(vps, self.ones, dg[:], True, True, [self.cst, dg], [vp])
                k.act(sb[:, h, :], sb[:, h, :], AF.Identity, [sb.p(h), BCa], [sb.p(h)], scale=BCa[:, b, h:h + 1])
                k.stt(sb[:, h, :], vps, kq[:, b, h // 2, 0:1], sb[:, h, :], ALU.mult, ALU.add, [vp, kq, sb.p(h)], [sb.p(h)])
            k.dma("sp", self.nds[jl, b].rearrange("h k v -> k h v"), sb[:, :, :], [sb], [self.nds], sb)
        k.tt("dve", sqs[:, :].rearrange("p (h b) -> p h b", b=NS), OB[:, :, :], OB[:, :, :], ALU.mult, [OB], [sqs])
        bank = self.mm_bank()
        k.mm(bank[:, 0:512], self.ones, sqs[:], True, True, [self.cst, sqs], [bank])
        k.act(sqs[:], bank[:, 0:512], AF.Sqrt, [bank, self.small], [sqs], bias=self.epsc, scale=1.0 / 128)
        k.recip(sqs[:], sqs[:], [sqs], [sqs])
        nwc = self.small[:, 4:5]
        k.tt("dve", dgs[0][:], lay[:, 64:192], self.ident, ALU.mult, [lay, self.cst], [dgs[0]])
        k.op("dve", lambda e, o=nwc, i=dgs[0][:]: e.tensor_reduce(out=o, in_=i, axis=AX.X, op=ALU.add), [dgs[0]], [self.small.p(4)])
        k.stt(sqs[:, :].rearrange("p (h b) -> p h b", b=NS), OB[:, :, :], nwc, sqs[:, :].rearrange("p (h b) -> p h b", b=NS),
              ALU.mult, ALU.mult, [OB, self.small.p(4), sqs], [sqs])
        k.tt("dve", oTs[:, :, :], sqs[:, :].rearrange("p (h b) -> p h b", b=NS), zT[:, :, :], ALU.mult, [sqs, zT], [oTs])
        k.barrier()
        RO.reset()
        outbuf = RO.alloc("outbuf", [128, 4, D], F32, 4)
        self.out_proj(self.g_w_out, jl, 32, oTs, 1, NS, outbuf)
        k.barrier()
        return outbuf


def make_consts():
    p = np.arange(128)[:, None]
    f = np.arange(128)[None, :]
    mats = [(p == f), -1.0 * (p > f), -1.0 * (p < f), (p <= f), (p == 127) * np.ones((128, 128)), np.ones((128, 128))]
    return np.ascontiguousarray(np.concatenate([np.asarray(m, dtype=np.float32) for m in mats], axis=1))


_GEN_CACHE = {}
_LAST_DBG = {}


def run_step(inp, NP, NS, DEPTH, ncores, nseq, stop_after=None):
    key = (NP, NS, DEPTH)
    if key not in _GEN_CACHE:
        _GEN_CACHE[key] = Gen(NP, NS, DEPTH, stop_after)
    gen = _GEN_CACHE[key]
    NA, NB = (DEPTH + 1) // 2, DEPTH // 2
    f = lambda a: np.ascontiguousarray(np.asarray(a, dtype=np.float32))
    cst = make_consts()
    shared = {}
    for n in ("w_ada", "b_ada", "norm_gain", "w_up", "w_down", "gdn_w_qkvz", "gdn_w_ba", "gdn_conv_w", "gdn_w_out", "sc_w_in", "sc_conv_w", "sc_w_out"):
        shared[n] = f(inp[n])
    if NB == 0:
        shared["sc_w_in"] = np.zeros((1, D, 3 * D), np.float32)
        shared["sc_conv_w"] = np.zeros((1, 3, D), np.float32)
        shared["sc_w_out"] = np.zeros((1, D, D), np.float32)
    gsm = np.zeros((NA, 1024), np.float32)
    gsm[:, 0:32] = f(inp["gdn_dt_bias"])
    gsm[:, 32:64] = f(inp["gdn_a_log"])
    gsm[:, 64:192] = f(inp["gdn_norm"])
    shared["gsm"] = gsm
    shared["cst"] = cst
    in_maps = []
    for c in range(ncores):
        sq = c % nseq
        sl = slice(c * NS, (c + 1) * NS)
        m = dict(shared)
        m["xp"] = f(inp["x_prompt"][sq])
        m["xs"] = f(inp["x_sample"][sl, 0, :])
        m["cv"] = f(np.concatenate([inp["c_sample"][sl], inp["c_prompt"][sq:sq + 1]], axis=0))
        m["sdel"] = f(inp["state_delta"][:, sl])
        m["sqkv"] = f(np.asarray(inp["state_qkv_conv"])[:, sl].reshape(NA, NS * 3, CONVD))
        if NB:
            m["ssc"] = f(np.asarray(inp["state_short_conv"])[:, sl].reshape(NB, NS * 2, D))
        else:
            m["ssc"] = np.zeros((1, NS * 2, D), np.float32)
        in_maps.append(m)
    res = run_bass_kernel_spmd(gen.nc, in_maps, core_ids=list(range(ncores)))
    r = res.results
    _LAST_DBG.clear()
    _LAST_DBG.update({n: r[0]["dbg_" + n] for n in gen.dbg})
    y_p = np.stack([r[c]["yp"] for c in range(nseq)], axis=0)
    y_s = np.concatenate([r[c]["ys"] for c in range(ncores)], axis=0)[:, None, :]
    nd_p = np.stack([r[c]["ndp"] for c in range(nseq)], axis=1)
    nq_p = np.stack([r[c]["nqp"] for c in range(nseq)], axis=1)
    ns_p = np.stack([r[c]["nsp"][:NB] for c in range(nseq)], axis=1)
    nd_s = np.concatenate([r[c]["nds"] for c in range(ncores)], axis=1)
    nq_s = np.concatenate([r[c]["nqs"].reshape(NA, NS, 3, CONVD) for c in range(ncores)], axis=1)
    ns_s = np.concatenate([r[c]["nss"][:NB].reshape(NB, NS, 2, D) for c in range(ncores)], axis=1)
    return tuple(np.ascontiguousarray(a, dtype=np.float32) for a in (y_p, y_s, nd_p, nq_p, ns_p, nd_s, nq_s, ns_s))


def kernel(**inputs):
    return run_step(inputs, 2048, 16, 4, 8, 4)
```
